# Optimizing a Trainium2 kernel written in Bass

```python
import math
import jax, jax.numpy as jnp
from jax import lax
import numpy as np

D_MODEL = 1024
BATCH = 32
SEQ = 256
DEPTH = 4
DEC_BATCH = 8
DEC_SEQ = 1024
PAST_LEN = 256

GRID_W = 64
HEAD_DIM = 64
N_HEADS_A = 8
N_HEADS_B = 8
D_A = N_HEADS_A * HEAD_DIM
D_B = N_HEADS_B * HEAD_DIM
WIN_R = 8
WIN_C = 16
Q_BLK_C = 16
K_BAND_C = 32
N_HEADS_C = 8
DK_C = D_MODEL // N_HEADS_C
DV_C = D_MODEL // N_HEADS_C
D_C = N_HEADS_C * DK_C
D_IN_EVEN = 3 * D_A + 4 * D_B
D_IN_ODD = 5 * D_C
D_FF = ((8 * D_MODEL // 3 + 255) // 256) * 256
RET_CHUNK = 64
HGRN_CHUNK = 32
ROPE_BASE = 10000.0
EPS = 1e-6
MASK_NEG = -1e30
F_MIN = 1e-30
N_EVEN = (DEPTH + 1) // 2
N_ODD = DEPTH // 2

kernel_name = "hybrid_na_retention_hgrn2_diffusion_step"

F32 = jnp.float32


def _rmsnorm(x, g=None):
    xf = x.astype(F32)
    y = xf * lax.rsqrt(jnp.mean(xf * xf, axis=-1, keepdims=True) + EPS)
    if g is not None:
        y = y * g.astype(F32)
    return y.astype(x.dtype)


def _heads(a, dh):
    b, t, _ = a.shape
    return a.reshape(b, t, -1, dh).transpose(0, 2, 1, 3)


def _merge(a):
    b, h, t, d = a.shape
    return a.transpose(0, 2, 1, 3).reshape(b, t, h * d)


def _flip(a):
    return jnp.flip(a, axis=2)


def _modulation(cvec, w_mod, b_mod):
    m = jax.nn.silu(cvec) @ w_mod + b_mod
    return jnp.split(m[..., None, :], 6, axis=-1)


def _axial_rope(x):
    n = x.shape[-2]
    t = jnp.arange(n)
    half = HEAD_DIM // 2
    nf = half // 2
    inv = ROPE_BASE ** (-jnp.arange(nf, dtype=F32) / nf)

    def rot(xa, pos):
        ang = pos.astype(F32)[:, None] * inv[None, :]
        cos, sin = jnp.cos(ang), jnp.sin(ang)
        x1, x2 = xa[..., :nf], xa[..., nf:]
        return jnp.concatenate([x1 * cos - x2 * sin, x1 * sin + x2 * cos], axis=-1)

    xf = x.astype(F32)
    out = jnp.concatenate([rot(xf[..., :half], t // GRID_W), rot(xf[..., half:], t % GRID_W)], axis=-1)
    return out.astype(x.dtype)


def _retention_scan(q, k, v, log_g, s0):
    b, h, t, dk = q.shape
    dv = v.shape[-1]
    c = RET_CHUNK
    n = t // c

    def chunks(a):
        return jnp.moveaxis(a.astype(F32).reshape(b, h, n, c, a.shape[-1]), 2, 0)

    pos = jnp.arange(c, dtype=F32)
    diff = pos[:, None] - pos[None, :]
    dmask = jnp.where(diff >= 0, jnp.exp(log_g[:, None, None] * jnp.maximum(diff, 0.0)), 0.0)
    q_dec = jnp.exp(log_g[:, None] * (pos + 1.0))[:, :, None]
    k_dec = jnp.exp(log_g[:, None] * (c - 1.0 - pos))[:, :, None]
    s_dec = jnp.exp(log_g * c)[:, None, None]

    def step(s, xs):
        qi, ki, vi = xs
        att = jnp.einsum('bhid,bhjd->bhij', qi, ki) * dmask
        o = jnp.einsum('bhij,bhjv->bhiv', att, vi) + jnp.einsum('bhid,bhdv->bhiv', qi * q_dec, s)
        s = s * s_dec + jnp.einsum('bhjd,bhjv->bhdv', ki * k_dec, vi)
        return s, o

    s_fin, o = lax.scan(step, s0.astype(F32), (chunks(q), chunks(k), chunks(v)))
    return jnp.moveaxis(o, 0, 2).reshape(b, h, t, dv), s_fin


def _gla_scan(q, k, v, log_f, s0):
    b, h, t, dk = q.shape
    dv = v.shape[-1]
    c = HGRN_CHUNK
    n = t // c

    def chunks(a):
        return jnp.moveaxis(a.astype(F32).reshape(b, h, n, c, a.shape[-1]), 2, 0)

    causal = jnp.tril(jnp.ones((c, c), dtype=bool))[:, :, None]

    def step(s, xs):
        qi, ki, vi, fi = xs
        bcum = jnp.cumsum(fi, axis=2)
        rel = bcum[:, :, :, None, :] - bcum[:, :, None, :, :]
        dec = jnp.where(causal, jnp.exp(jnp.minimum(rel, 0.0)), 0.0)
        att = jnp.einsum('bhtd,bhsd,bhtsd->bhts', qi, ki, dec)
        o = jnp.einsum('bhts,bhsv->bhtv', att, vi) + jnp.einsum('bhtd,bhdv->bhtv', qi * jnp.exp(bcum), s)
        blast = bcum[:, :, -1:, :]
        s = s * jnp.exp(blast[:, :, 0, :, None]) + jnp.einsum('bhsd,bhsv->bhdv', ki * jnp.exp(blast - bcum), vi)
        return s, o

    s_fin, o = lax.scan(step, s0.astype(F32), (chunks(q), chunks(k), chunks(v), chunks(log_f)))
    return jnp.moveaxis(o, 0, 2).reshape(b, h, t, dv), s_fin


def _even_project(h, w_in):
    z = h @ w_in
    parts = jnp.split(z, [D_A, 2 * D_A, 3 * D_A, 3 * D_A + D_B, 3 * D_A + 2 * D_B, 3 * D_A + 3 * D_B], axis=-1)
    return [_heads(p, HEAD_DIM) for p in parts]


def _attend_context(q, k, v):
    s = jnp.einsum('bhqd,bhkd->bhqk', q, k).astype(F32) * (HEAD_DIM ** -0.5)
    p = jax.nn.softmax(s, axis=-1).astype(v.dtype)
    return jnp.einsum('bhqk,bhkd->bhqd', p, v)


def _neighbourhood_attend(q, k, v, k_ctx, v_ctx, rpb):
    b, h, n, dh = q.shape
    rows = n // GRID_W
    wr = min(WIN_R, rows)
    nblk = GRID_W // Q_BLK_C
    r = jnp.arange(rows)
    row_idx = jnp.clip(r - wr // 2, 0, rows - wr)[:, None] + jnp.arange(wr)[None, :]
    qcol = jnp.arange(GRID_W).reshape(nblk, Q_BLK_C)
    band0 = jnp.clip(qcol[:, 0] - WIN_C // 2, 0, GRID_W - K_BAND_C)
    col_idx = band0[:, None] + jnp.arange(K_BAND_C)[None, :]
    win0 = jnp.clip(qcol - WIN_C // 2, 0, GRID_W - WIN_C)
    valid = (col_idx[:, None, :] >= win0[:, :, None]) & (col_idx[:, None, :] < win0[:, :, None] + WIN_C)
    row_off = row_idx - r[:, None] + (WIN_R - 1)
    col_off = jnp.clip(col_idx[:, None, :] - qcol[:, :, None], 1 - WIN_C, WIN_C - 1) + (WIN_C - 1)
    bias = rpb.astype(F32)[:, row_off[:, None, None, :, None], col_off[None, :, :, None, :]]

    def band(a):
        return a.reshape(b, h, rows, GRID_W, dh)[:, :, row_idx][:, :, :, :, col_idx]

    scale = HEAD_DIM ** -0.5
    qg = q.reshape(b, h, rows, nblk, Q_BLK_C, dh)
    s_loc = jnp.einsum('bhrjqd,bhrwjkd->bhrjqwk', qg, band(k)).astype(F32) * scale + bias[None]
    s_loc = jnp.where(valid[:, :, None, :], s_loc, MASK_NEG)
    s_ctx = jnp.einsum('bhrjqd,bhpd->bhrjqp', qg, k_ctx).astype(F32) * scale
    nl = wr * K_BAND_C
    s = jnp.concatenate([s_loc.reshape(b, h, rows, nblk, Q_BLK_C, nl), s_ctx], axis=-1)
    p = jax.nn.softmax(s, axis=-1).astype(v.dtype)
    p_loc = p[..., :nl].reshape(b, h, rows, nblk, Q_BLK_C, wr, K_BAND_C)
    o = (jnp.einsum('bhrjqwk,bhrwjkd->bhrjqd', p_loc, band(v))
         + jnp.einsum('bhrjqp,bhpd->bhrjqd', p[..., nl:], v_ctx))
    return o.reshape(b, h, n, dh)


def _retention_bidir(q, k, v, g, ret_decay, s0):
    log_g = -jnp.exp(ret_decay.astype(F32))
    k = k * (HEAD_DIM ** -0.5)
    o_f, s_f = _retention_scan(q, k, v, log_g[0], s0[:, 0])
    o_b, s_b = _retention_scan(_flip(q), _flip(k), _flip(v), log_g[1], s0[:, 1])
    o = _rmsnorm(o_f + _flip(o_b)) * jax.nn.silu(g.astype(F32))
    return o.astype(q.dtype), jnp.stack([s_f, s_b], axis=1)


def _hgrn_bidir(h, w_in, w_out, lb, gnorm, s0):
    z = h @ w_in
    q, f_fw, f_bw, i, g = [_heads(p, DK_C) for p in jnp.split(z, 5, axis=-1)]
    q = jax.nn.silu(q.astype(F32)) * (DK_C ** -0.5)

    def gates(fr, lbd):
        lbh = lbd.reshape(N_HEADS_C, 1, DK_C)
        frf = fr.astype(F32)
        f = lbh + (1.0 - lbh) * jax.nn.sigmoid(frf)
        inp = (1.0 - lbh) * jax.nn.sigmoid(-frf)
        return jnp.log(jnp.maximum(f, F_MIN)), inp

    lf_f, k_f = gates(f_fw, lb[0])
    lf_b, k_b = gates(f_bw, lb[1])
    v = i.astype(F32)
    o_f, s_f = _gla_scan(q, k_f, v, lf_f, s0[:, 0])
    o_b, s_b = _gla_scan(_flip(q), _flip(k_b), _flip(v), _flip(lf_b), s0[:, 1])
    o = _rmsnorm(o_f + _flip(o_b), gnorm) * jax.nn.silu(g.astype(F32))
    return _merge(o.astype(h.dtype)) @ w_out, jnp.stack([s_f, s_b], axis=1)


def _swiglu(h, w_in, w_out):
    a, u = jnp.split(h @ w_in, 2, axis=-1)
    return (jax.nn.silu(a) * u) @ w_out


def setup_inputs(seed: int = 0) -> dict:
    key = jax.random.key(seed)
    ks = jax.random.split(key, 20)

    def nrm(k, shape, s):
        return jax.random.normal(k, shape, F32) * s

    ret_base = jnp.log(-jnp.log(1.0 - jnp.power(2.0, -5.0 - jnp.arange(N_HEADS_B, dtype=F32))))
    return {
        "x_prompt": nrm(ks[0], (BATCH, SEQ, D_MODEL), 1.0),
        "x_sample": nrm(ks[1], (DEC_BATCH, DEC_SEQ, D_MODEL), 1.0),
        "cache_kv": nrm(ks[2], (DEC_BATCH, N_EVEN, 2, N_HEADS_A, PAST_LEN, HEAD_DIM), 1.0),
        "state_ret": nrm(ks[3], (DEC_BATCH, N_EVEN, 2, N_HEADS_B, HEAD_DIM, HEAD_DIM), 0.5),
        "state_hgrn": nrm(ks[4], (DEC_BATCH, N_ODD, 2, N_HEADS_C, DK_C, DV_C), 0.5),
        "c": nrm(ks[5], (DEC_BATCH, D_MODEL), 1.0),
        "c_ctx": nrm(ks[6], (D_MODEL,), 1.0),
        "w_mod": nrm(ks[7], (DEPTH, D_MODEL, 6 * D_MODEL), 0.5 * D_MODEL ** -0.5),
        "b_mod": nrm(ks[8], (DEPTH, 6 * D_MODEL), 0.02),
        "norm_g": 1.0 + nrm(ks[9], (DEPTH, 4, D_MODEL), 0.02),
        "w_in_even": nrm(ks[10], (N_EVEN, D_MODEL, D_IN_EVEN), D_MODEL ** -0.5),
        "w_out_even": nrm(ks[11], (N_EVEN, D_A + D_B, D_MODEL), (D_A + D_B) ** -0.5),
        "rpb": nrm(ks[12], (N_EVEN, N_HEADS_A, 2 * WIN_R - 1, 2 * WIN_C - 1), 0.1),
        "ret_decay": ret_base + nrm(ks[13], (N_EVEN, 2, N_HEADS_B), 0.1),
        "w_in_odd": nrm(ks[14], (N_ODD, D_MODEL, D_IN_ODD), D_MODEL ** -0.5),
        "w_out_odd": nrm(ks[15], (N_ODD, D_C, D_MODEL), D_C ** -0.5),
        "hgrn_lb": nrm(ks[16], (N_ODD, 2, D_C), 1.0),
        "hgrn_gnorm": 1.0 + nrm(ks[17], (N_ODD, DV_C), 0.02),
        "w_ffn_in": nrm(ks[18], (DEPTH, D_MODEL, 2 * D_FF), D_MODEL ** -0.5),
        "w_ffn_out": nrm(ks[19], (DEPTH, D_FF, D_MODEL), D_FF ** -0.5),
    }


def reference(x_prompt, x_sample, cache_kv, state_ret, state_hgrn, c, c_ctx, w_mod, b_mod, norm_g,
              w_in_even, w_out_even, rpb, ret_decay, w_in_odd, w_out_odd, hgrn_lb, hgrn_gnorm,
              w_ffn_in, w_ffn_out):
    p_lb = jax.nn.softmax(hgrn_lb.astype(F32), axis=0)
    lower = jnp.clip(jnp.cumsum(p_lb, axis=0) - p_lb[0], 0.0, 1.0)
    bp = x_prompt.shape[0]
    xp, xs = x_prompt, x_sample
    kv_states, ret_states, hg_states = [], [], []
    for l in range(DEPTH):
        sh1p, sc1p, gt1p, sh2p, sc2p, gt2p = _modulation(c_ctx, w_mod[l], b_mod[l])
        sh1s, sc1s, gt1s, sh2s, sc2s, gt2s = _modulation(c, w_mod[l], b_mod[l])
        hp = _rmsnorm(xp, norm_g[l, 0]) * (1.0 + sc1p) + sh1p
        hs = _rmsnorm(xs, norm_g[l, 0]) * (1.0 + sc1s) + sh1s
        if l % 2 == 0:
            e = l // 2
            qa, ka, va, qb, kb, vb, gb = _even_project(hp, w_in_even[e])
            oa = _attend_context(qa, ka, va)
            zero_ret = jnp.zeros((bp, 2, N_HEADS_B, HEAD_DIM, HEAD_DIM), F32)
            ob, s_ret = _retention_bidir(qb, kb, vb, gb, ret_decay[e], zero_ret)
            mp = jnp.concatenate([_merge(oa), _merge(ob)], axis=-1) @ w_out_even[e]
            kv_states.append(jnp.stack([ka, va], axis=1))
            ret_states.append(s_ret)
            qa, ka, va, qb, kb, vb, gb = _even_project(hs, w_in_even[e])
            oa = _neighbourhood_attend(qa, ka, va, cache_kv[:, e, 0], cache_kv[:, e, 1], rpb[e])
            ob, _ = _retention_bidir(_axial_rope(qb), _axial_rope(kb), vb, gb, ret_decay[e], state_ret[:, e])
            ms = jnp.concatenate([_merge(oa), _merge(ob)], axis=-1) @ w_out_even[e]
        else:
            o_i = l // 2
            zero_hg = jnp.zeros((bp, 2, N_HEADS_C, DK_C, DV_C), F32)
            mp, s_hg = _hgrn_bidir(hp, w_in_odd[o_i], w_out_odd[o_i], lower[o_i], hgrn_gnorm[o_i], zero_hg)
            hg_states.append(s_hg)
            ms, _ = _hgrn_bidir(hs, w_in_odd[o_i], w_out_odd[o_i], lower[o_i], hgrn_gnorm[o_i], state_hgrn[:, o_i])
        xp = xp + gt1p * _rmsnorm(mp, norm_g[l, 1])
        xs = xs + gt1s * _rmsnorm(ms, norm_g[l, 1])
        hp = _rmsnorm(xp, norm_g[l, 2]) * (1.0 + sc2p) + sh2p
        hs = _rmsnorm(xs, norm_g[l, 2]) * (1.0 + sc2s) + sh2s
        xp = xp + gt2p * _rmsnorm(_swiglu(hp, w_ffn_in[l], w_ffn_out[l]), norm_g[l, 3])
        xs = xs + gt2s * _rmsnorm(_swiglu(hs, w_ffn_in[l], w_ffn_out[l]), norm_g[l, 3])
    new_cache_kv = jnp.stack(kv_states, axis=1).astype(x_prompt.dtype)
    new_state_ret = jnp.stack(ret_states, axis=1).astype(x_prompt.dtype)
    new_state_hgrn = jnp.stack(hg_states, axis=1).astype(x_prompt.dtype)
    return (xp, xs, new_cache_kv, new_state_ret, new_state_hgrn)
```

```python
import contextlib
import numpy as np
import concourse.bass as bass
import concourse.mybir as mybir
from concourse.bass_utils import run_bass_kernel_spmd

F32 = mybir.dt.float32
BF16 = mybir.dt.bfloat16
AF = mybir.ActivationFunctionType
ALU = mybir.AluOpType

ENGS = ("pe", "dve", "act", "pool", "sp")
D = 1024
DFF = 2816
EPS = 1e-6
XC = 896
W2W = 1920


class Op:
    __slots__ = ("eng", "fn", "deps", "inc", "semval", "dma_sem", "dma_val")

    def __init__(self, eng, fn):
        self.eng = eng
        self.fn = fn
        self.deps = set()
        self.inc = False
        self.semval = None
        self.dma_sem = None
        self.dma_val = None


class Prog:
    def __init__(self, nc):
        self.nc = nc
        self.ops = {e: [] for e in ENGS}
        self.last_w = {}
        self.readers = {}
        self.dma_sems = {}

    @staticmethod
    def _bank(k):
        if k == "psb":
            return 7
        if isinstance(k, tuple):
            if k[0] == "ps":
                return k[1]
            if k[0] == "pso":
                return 5
            if k[0] == "psd":
                return 6
        return None

    @staticmethod
    def _canon(keys):
        out = []
        for k in keys:
            if k == ("ps", 5):
                out += [("pso", 0), ("pso", 1)]
            elif k == ("ps", 6):
                out += [("psd", 0), ("psd", 1)]
            else:
                out.append(k)
        return out

    def op(self, eng, fn, reads=(), writes=(), dma_tag=None):
        o = Op(eng, fn)
        reads = self._canon(reads)
        writes = self._canon(writes)
        if eng != "pe":
            extra = [("rx", self._bank(k)) for k in reads if self._bank(k) is not None]
            if extra:
                writes = list(writes) + extra
        for k in reads:
            w = self.last_w.get(k)
            if w is not None:
                o.deps.add(w)
        for k in writes:
            w = self.last_w.get(k)
            if w is not None:
                o.deps.add(w)
            for r in self.readers.get(k, ()):
                o.deps.add(r)
        o.deps.discard(o)
        for k in reads:
            self.readers.setdefault(k, []).append(o)
        for k in writes:
            self.last_w[k] = o
            self.readers[k] = []
        if dma_tag is not None:
            cnt = self.dma_sems.setdefault(dma_tag, [0])
            cnt[0] += 16
            o.dma_sem = dma_tag
            o.dma_val = cnt[0]
        self.ops[eng].append(o)
        return o

    @staticmethod
    def _skip(o, d):
        return d.dma_sem is None and d.eng == o.eng and o.eng == "pe" and o.dma_sem is None

    def emit(self):
        nc = self.nc
        for e in ENGS:
            for o in self.ops[e]:
                for d in o.deps:
                    if d.dma_sem is not None or self._skip(o, d):
                        continue
                    d.inc = True
        for e in ENGS:
            c = 0
            for o in self.ops[e]:
                if o.dma_sem is None and o.inc:
                    c += 1
                    o.semval = c
        with contextlib.ExitStack() as st:
            esem = {e: st.enter_context(nc.semaphore("s_" + e)) for e in ENGS}
            dsem = {t: st.enter_context(nc.semaphore("d%d" % i)) for i, t in enumerate(self.dma_sems)}
            block = st.enter_context(nc.Block())

            def run_engine(e, eobj):
                waited = {}
                for o in self.ops[e]:
                    need = {}
                    for d in o.deps:
                        if d.dma_sem is not None:
                            key = ("d", d.dma_sem)
                            need[key] = max(need.get(key, 0), d.dma_val)
                        else:
                            if self._skip(o, d):
                                continue
                            key = ("e", d.eng)
                            need[key] = max(need.get(key, 0), d.semval)
                    for key, v in need.items():
                        if waited.get(key, 0) >= v:
                            continue
                        waited[key] = v
                        s = dsem[key[1]] if key[0] == "d" else esem[key[1]]
                        eobj.wait_ge(s, v)
                    ins = o.fn(eobj)
                    if o.dma_sem is not None:
                        ins.then_inc(dsem[o.dma_sem], 16)
                    elif o.inc:
                        ins.then_inc(esem[e], 1)
                if e == "sp":
                    for t, cnt in self.dma_sems.items():
                        eobj.wait_ge(dsem[t], cnt[0])
                    for e2 in ENGS:
                        if e2 == "sp":
                            continue
                        last = None
                        for o2 in self.ops[e2]:
                            if o2.semval is not None:
                                last = o2.semval
                        if last:
                            eobj.wait_ge(esem[e2], last)

            block.tensor(lambda eng: run_engine("pe", eng))
            block.vector(lambda eng: run_engine("dve", eng))
            block.scalar(lambda eng: run_engine("act", eng))
            block.gpsimd(lambda eng: run_engine("pool", eng))
            block.sync(lambda eng: run_engine("sp", eng))


def _host_consts():
    c = {}
    c["ident"] = np.eye(128, dtype=np.float32)
    c["onesmean"] = np.full((128, 128), 1.0 / 1024, np.float32)
    c["om128"] = np.full((128, 128), 1.0 / 128, np.float32)
    bd = np.zeros((128, 128), np.float32)
    bd[:64, :64] = 1.0 / 64
    bd[64:, 64:] = 1.0 / 64
    c["bd64"] = bd
    c["ones"] = np.ones((128, 128), np.float32)
    oz = np.zeros((2, 128, 128), np.float32)
    oz[0, :, :64] = 1.0
    oz[1, :, 64:] = 1.0
    c["onesz0"] = oz[0]
    c["onesz1"] = oz[1]
    rt = np.zeros((128, 128), np.float32)
    for i in range(128):
        if i % 32 < 16:
            rt[i + 16, i] = -1.0
        else:
            rt[i - 16, i] = 1.0
    c["rotT"] = rt
    t = np.arange(1024)
    inv = (10000.0 ** (-np.arange(16, dtype=np.float32) / 16)).astype(np.float32)
    cs = np.zeros((128, 1024), np.float32)
    sn = np.zeros((128, 1024), np.float32)
    for i in range(128):
        d = i % 64
        pos = (t // 64) if d < 32 else (t % 64)
        ang = pos.astype(np.float32) * inv[d % 16]
        cs[i] = np.cos(ang)
        sn[i] = np.sin(ang)
    c["cos"] = cs
    c["sin"] = sn
    ki = np.arange(128)[:, None]
    x = np.arange(W2W)[None, :]
    delta = (x - ki - XC).astype(np.float32)
    c["dpp"] = np.where(delta >= 0, delta, 1e7).astype(np.float32)
    c["dnn"] = np.where(delta <= 0, -delta, 1e7).astype(np.float32)
    c["ef"] = np.broadcast_to((np.arange(1024, dtype=np.float32) + 1.0)[None, :], (128, 1024)).copy()
    m = (np.arange(2)[None, :] * 128 + np.arange(128)[:, None]).astype(np.float32)
    c["epd"] = np.stack([255.0 - m, m], axis=1).astype(np.float32)
    p = np.arange(128)[:, None] % 32
    tt = np.arange(32)[None, :]
    c["hmask"] = np.stack([(tt >= p), (tt <= p)], axis=1).astype(np.float32)
    return c


def _nabias(rpb):
    out = np.empty((2, 8, 2, 6, 128, 512), np.float32)
    flat = np.concatenate([rpb.reshape(2, 8, -1), np.full((2, 8, 1), -1e30, np.float32)], axis=-1)
    PAD = 15 * 31
    for hf in range(2):
        for kt in range(6):
            pr = 2 * hf + kt
            kp = np.arange(128)
            kr = (2 * pr + kp // 64)[:, None]
            kc = (kp % 64)[:, None]
            ql = np.arange(512)
            r = (8 * hf + ql // 64)[None, :]
            cc = (ql % 64)[None, :]
            rs = np.clip(r - 4, 0, 8)
            rowv = (kr >= rs) & (kr < rs + 8)
            win0 = np.clip(cc - 8, 0, 48)
            colv = (kc >= win0) & (kc < win0 + 16)
            ro = kr - r + 7
            co = np.clip(kc - cc, -15, 15) + 15
            idx = np.where(rowv & colv, np.clip(ro, 0, 14) * 31 + co, PAD)
            out[:, :, hf, kt] = flat[:, :, idx]
    return out


def build(depth=4, groups=(0, 1), parts=("mod", "pre", "mixer", "ffn")):
    nc = bass.Bass("TRN2", target_bir_lowering=False)
    dt = lambda n, s, kind="ExternalInput": nc.dram_tensor(n, list(s), F32, kind=kind).ap()
    xin = dt("xin", [2, 1024, D])
    ckv = dt("ckv", [2, 2, 8, 256, 64])
    sret = dt("sret", [2, 2, 8, 64, 64])
    shg = dt("shg", [2, 2, 8, 128, 128])
    cvec = dt("cvec", [2, D])
    w_mod = dt("w_mod", [4, D, 6 * D])
    b_mod = dt("b_mod", [4, 6 * D])
    norm_g = dt("norm_g", [4, 4, D])
    w_in_even = dt("w_in_even", [2, D, 3584])
    w_out_even = dt("w_out_even", [2, D, D])
    ret_decay = dt("ret_decay", [32])
    w_in_odd = dt("w_in_odd", [2, D, 5120])
    w_out_odd = dt("w_out_odd", [2, D, D])
    hgrn_lb = dt("hgrn_lb", [2, 2, D])
    hgrn_gnorm = dt("hgrn_gnorm", [2, 128])
    w_ffn_in = dt("w_ffn_in", [4, D, 2 * DFF])
    w_ffn_out = dt("w_ffn_out", [4, DFF, D])
    nab = dt("nab", [2, 8, 2, 6, 128, 512])
    hc = {k: dt("c_" + k, v.shape) for k, v in _host_consts().items()}
    yout = dt("yout", [2, 1024, D], "ExternalOutput")
    nkv = dt("nkv", [4, 2, 2, 8, 256, 64], "ExternalOutput")
    nret = dt("nret", [4, 2, 2, 8, 64, 64], "ExternalOutput")
    nhg = dt("nhg", [4, 2, 2, 8, 128, 128], "ExternalOutput")

    st = contextlib.ExitStack()
    _n = [0]

    def sb(shape, dtype=F32, name=None):
        _n[0] += 1
        return st.enter_context(nc.sbuf_tensor("t%d_%s" % (_n[0], name or ""), list(shape), dtype))

    P = Prog(nc)
    st.enter_context(nc.allow_non_contiguous_dma(reason="small transposed parameter loads"))

    XT = sb([128, 8, 1024], F32, "XT")
    Hh = sb([128, 8192], BF16, "H")
    H_bf = Hh[:].rearrange("p (c t) -> p c t", t=1024)
    H_f = Hh[:].bitcast(F32).rearrange("p (m t) -> p m t", t=512)
    Uu = sb([128, 22 * 1024], BF16, "U")
    U_bf = Uu[:].rearrange("p (j t) -> p j t", t=1024)
    U_f = Uu[:].bitcast(F32)
    wbuf = [sb([128, 4096], BF16, "wbuf%d" % i) for i in range(3)]
    sq = sb([128, 8, 512], BF16, "sq")
    pT2 = sb([128, 4096], BF16, "pT")
    pTt = pT2[:].rearrange("p (i t) -> p i t", t=512)
    pT_f = pT2[:].bitcast(F32).rearrange("p (s t) -> p s t", t=1024)
    rt_t = sb([128, 512], F32, "rt")
    rstd_t = sb([128, 512], F32, "rstd")
    tmp_t = [sb([128, 512], F32, "tmp%d" % i) for i in range(2)]
    eps_t = sb([128, 1], F32, "eps")
    epoch_t = sb([128, 1], F32, "epoch")
    nln8 = sb([128, 1], F32, "nln8")
    ident_f = sb([128, 128], F32, "identf")
    ident_b = sb([128, 128], BF16, "identb")
    onesmean = sb([128, 128], BF16, "onesmean")
    om128 = sb([128, 128], BF16, "om128")
    bd64 = sb([128, 128], BF16, "bd64")
    ones_b = sb([128, 128], BF16, "ones")
    onesz = [sb([128, 128], BF16, "onesz%d" % i) for i in range(2)]
    rotT = sb([128, 128], BF16, "rotT")
    hmask = sb([128, 2, 32], BF16, "hmask")
    epd = sb([128, 2, 2], F32, "epd")
    cT = sb([128, 2, 8], F32, "cT")
    scT = sb([128, 2, 8], BF16, "scT")
    bmT = sb([128, 4, 48], F32, "bmT")
    ngT = sb([128, 4, 4, 8], F32, "ngT")
    modv = sb([128, 4, 48, 2], F32, "modv")
    modd = sb([128, 4, 6, 8, 2], F32, "modd")
    mtmp = sb([128, 8, 2], F32, "mtmp")
    rd = sb([128, 32], F32, "rd")
    lg = sb([128, 32], F32, "lg")
    nlg = sb([128, 32], F32, "nlg")
    lg1025 = sb([128, 32], F32, "lg1025")
    lbT = sb([128, 2, 2, 8], F32, "lbT")
    lowt = sb([128, 2, 2, 8], F32, "lowt")
    omlt = sb([128, 2, 2, 8], F32, "omlt")
    nomlt = sb([128, 2, 2, 8], F32, "nomlt")
    gnT = sb([128, 2], F32, "gnT")
    ARENA = 59 * 1024
    arena = sb([128, ARENA // 2], BF16, "arena")

    ps = [st.enter_context(nc.psum_tensor("ps%d" % i, [128, 512], F32)) for i in range(7)]
    psb = st.enter_context(nc.psum_tensor("psb", [128, 1024], BF16))
    _rr = [0]

    def nextps():
        i = _rr[0] % 5
        _rr[0] += 1
        return ps[i], ("ps", i)

    PSO, PSD = ps[5], ps[6]

    class Carver:
        def __init__(self):
            self.off = 0

        def take(self, nbytes, dtype, shape_tail=None):
            assert self.off % 4 == 0
            n_el = nbytes // (2 if dtype == BF16 else 4)
            a = arena[:, self.off // 2: self.off // 2 + nbytes // 2]
            if dtype == F32:
                a = a.bitcast(F32)
            self.off += nbytes
            assert self.off <= ARENA, self.off
            return a

    def dma_sp(out, in_, reads, writes, tag):
        P.op("sp", lambda e: e.dma_start(out=out, in_=in_), reads=reads, writes=writes, dma_tag=tag)

    def dma_pool(out, in_, reads, writes, tag):
        P.op("pool", lambda e: e.dma_start(out=out, in_=in_), reads=reads, writes=writes, dma_tag=tag)

    def act(out, in_, func, reads, writes, scale=None, bias=None):
        kw = {}
        if scale is not None:
            kw["scale"] = scale
        if bias is not None:
            kw["bias"] = bias
        P.op("act", lambda e: e.activation(out=out, in_=in_, func=func, **kw), reads=reads, writes=writes)

    def mm(out, lhsT, rhs, start, stop, reads, writes, tp=None):
        if tp is None:
            P.op("pe", lambda e: e.matmul(out, lhsT=lhsT, rhs=rhs, start=start, stop=stop), reads=reads, writes=writes)
        else:
            P.op("pe", lambda e: e.matmul(out, lhsT=lhsT, rhs=rhs, start=start, stop=stop, tile_position=tp), reads=reads, writes=writes)

    def dve(fn, reads, writes):
        P.op("dve", fn, reads=reads, writes=writes)

    _wi = [0]

    def load_w(pieces, kc):
        i = _wi[0] % 3
        _wi[0] += 1
        ntot = sum(p.shape[1] for p in pieces)
        assert kc * ntot <= 4096
        view = wbuf[i][:, 0:kc * ntot].rearrange("p (k n) -> p k n", n=ntot)
        off = 0
        for p in pieces:
            n = p.shape[1]
            src = p.rearrange("(k p) n -> p k n", p=128)
            dma_pool(view[:, :, off:off + n], src, [], [("wb", i)], ("wb", i))
            off += n
        return view, ("wb", i)

    cl = 0

    def cload(dst, src, key, cast=False):
        nonlocal cl
        cl += 1
        if cast:
            dma_pool(dst, src, [], [key], ("c", cl))
        else:
            dma_sp(dst, src, [], [key], ("c", cl))

    cload(ident_f[:], hc["ident"], "identf")
    cload(ident_b[:], hc["ident"], "identb", True)
    cload(onesmean[:], hc["onesmean"], "onesmean", True)
    cload(om128[:], hc["om128"], "om128", True)
    cload(bd64[:], hc["bd64"], "bd64", True)
    cload(ones_b[:], hc["ones"], "ones", True)
    dma_pool(onesz[0][:], hc["onesz0"], [], ["onesz"], ("c", "onesz"))
    dma_pool(onesz[1][:], hc["onesz1"], [], ["onesz"], ("c", "onesz"))
    cload(rotT[:], hc["rotT"], "rotT", True)
    cload(hmask[:], hc["hmask"], "hmask", True)
    cload(epd[:], hc["epd"], "epd")
    for j in range(2):
        dma_sp(cT[:, j, :], cvec[j].rearrange("(k p) -> p k", p=128), [], ["cT"], ("c", "cT"))
    for l in range(4):
        dma_sp(bmT[:, l, :], b_mod[l].rearrange("(m p) -> p m", p=128), [], ["bmT"], ("c", "bmT"))
        for i in range(4):
            dma_sp(ngT[:, l, i, :], norm_g[l, i].rearrange("(c p) -> p c", p=128), [], ["ngT"], ("c", "ngT"))
    for o in range(2):
        for d_ in range(2):
            dma_sp(lbT[:, o, d_, :], hgrn_lb[o, d_].rearrange("(h p) -> p h", p=128), [], ["lbT"], ("c", "lbT"))
    cload(rd[:], ret_decay.partition_broadcast(128), "rd")
    cload(gnT[:], hgrn_gnorm.rearrange("o p -> p o"), "gnT")
    dve(lambda e: e.memset(eps_t[:], EPS), [], ["eps"])
    dve(lambda e: e.memset(nln8[:], -2.0794415416798357), ["eps"], ["eps"])
    act(lg[:], rd[:], AF.Exp, ["rd"], ["lg"])
    dve(lambda e: e.tensor_scalar_mul(nlg[:], lg[:], 1.0), ["lg"], ["nlg"])
    dve(lambda e: e.tensor_scalar_mul(lg[:], nlg[:], -1.0), ["nlg", "lg"], ["lg"])
    dve(lambda e: e.tensor_scalar_mul(lg1025[:], lg[:], 1025.0), ["lg"], ["lg1025"])
    dve(lambda e: e.memset(lowt[:], 0.0), [], ["lowt"])
    dve(lambda e: e.tensor_sub(lowt[:, 1], lbT[:, 1], lbT[:, 0]), ["lbT", "lowt"], ["lowt"])
    act(lowt[:, 1], lowt[:, 1], AF.Sigmoid, ["lowt"], ["lowt"])
    dve(lambda e: e.memset(lowt[:, 0], 0.0), ["lowt"], ["lowt"])
    dve(lambda e: e.tensor_scalar(omlt[:], lowt[:], -1.0, 1.0, op0=ALU.mult, op1=ALU.add), ["lowt"], ["omlt"])
    dve(lambda e: e.tensor_scalar_mul(nomlt[:], omlt[:], -1.0), ["omlt"], ["nomlt"])

    act(scT[:], cT[:], AF.Silu, ["cT"], ["scT"])

    def mod_tile(l, mq):
        pm, pmk = PSD, "psd_mod"
        wt, wk = load_w([w_mod[l][:, mq * 512:(mq + 1) * 512]], 8)
        for mi in range(4):
            mc = mq * 4 + mi
            for k in range(8):
                mm(pm[:, mc * 2:mc * 2 + 2], wt[:, k, mi * 128:(mi + 1) * 128], scT[:, :, k], k == 0, k == 7,
                   [wk, "scT"], [("psd", 0), ("psd", 1)])

    def mod_finish(l):
        pm = PSD
        pmk2 = [("psd", 0), ("psd", 1)]
        dve(lambda e, l=l, pm=pm: e.tensor_tensor(out=modv[:, l], in0=pm[:, 0:96].rearrange("p (m j) -> p m j", j=2),
                                                  in1=bmT[:, l, :].unsqueeze(2).to_broadcast([128, 48, 2]), op=ALU.add),
            pmk2 + ["bmT"], [("modv", l)])
        for idx, (kind, mo, gi) in enumerate([("a", 8, 0), ("b", 0, None), ("g", 16, 1), ("a", 32, 2), ("b", 24, None), ("g", 40, 3)]):
            src = modv[:, l, mo:mo + 8, :]
            dst = modd[:, l, idx]
            if kind == "b":
                dve(lambda e, dst=dst, src=src: e.tensor_copy(out=dst, in_=src), [("modv", l)], [("modd", l, idx)])
            else:
                gb = ngT[:, l, gi, :].unsqueeze(2).to_broadcast([128, 8, 2])
                if kind == "a":
                    dve(lambda e, src=src: e.tensor_scalar_add(mtmp[:], src, 1.0), [("modv", l)], ["mtmp"])
                    dve(lambda e, dst=dst, gb=gb: e.tensor_tensor(out=dst, in0=mtmp[:], in1=gb, op=ALU.mult), ["mtmp", "ngT"], [("modd", l, idx)])
                else:
                    dve(lambda e, dst=dst, gb=gb, src=src: e.tensor_tensor(out=dst, in0=src, in1=gb, op=ALU.mult), [("modv", l), "ngT"], [("modd", l, idx)])

    def XTk(c, tb):
        return ("XT", c, tb)

    def XTa(c, tb):
        return XT[:, c, tb * 512:(tb + 1) * 512]

    def Hk(c, tb):
        return ("H", 2 * c + tb)

    def Ha(c, tb):
        return H_bf[:, c, tb * 512:(tb + 1) * 512]

    def Uk(j, tb):
        return ("U", 2 * j + tb)

    def Ua(j, tb):
        return U_bf[:, j, tb * 512:(tb + 1) * 512]

    def rms_stats(ones_mat, ones_key, nchunks, sqkeys):
        pa, pak = nextps()
        for c in range(nchunks):
            mm(pa[:], ones_mat[:], sq[:, c, :], c == 0, c == nchunks - 1, [ones_key, sqkeys[c]], [pak])
        act(rt_t[:], pa[:], AF.Ln, [pak, "eps"], ["rt"], bias=eps_t[:, 0:1])
        act(rstd_t[:], rt_t[:], AF.Exp, ["rt"], ["rstd"], scale=-0.5)

    def prenorm(l, which, g, tbs=(0, 1)):
        ai, bi = (0, 1) if which == 0 else (3, 4)
        for tb in tbs:
            for c in range(8):
                act(sq[:, c, :], XTa(c, tb), AF.Square, [XTk(c, tb)], [("sq", c)])
            rms_stats(onesmean, "onesmean", 8, [("sq", c) for c in range(8)])
            for c in range(8):
                t = tmp_t[c % 2]
                tk = ("tmp", c % 2)
                dve(lambda e, t=t, c=c, tb=tb: e.tensor_tensor(out=t[:], in0=XTa(c, tb), in1=rstd_t[:], op=ALU.mult),
                    [XTk(c, tb), "rstd"], [tk])
                act(Ha(c, tb), t[:], AF.Identity, [tk, ("modd", l, ai), ("modd", l, bi)], [Hk(c, tb)],
                    scale=modd[:, l, ai, c, g:g + 1], bias=modd[:, l, bi, c, g:g + 1])

    def outproj_postnorm(l, g, wap, kc, srcfn, ggi, in_u=False, after_tb=None):
        if in_u:
            mb = lambda m: U_f[:, 7168 + m * 512:7168 + (m + 1) * 512]
            mbk = lambda m: [("U", 28 + 2 * m), ("U", 29 + 2 * m)]
        else:
            mb = lambda m: H_f[:, m, :]
            mbk = lambda m: [("H", 2 * m), ("H", 2 * m + 1)]
        for tb in range(2):
            for m in range(8):
                wt, wk = load_w([wap[:, m * 128:(m + 1) * 128]], kc)
                pm, pmk = nextps()
                for k in range(kc):
                    sa, sk = srcfn(k, tb)
                    mm(pm[:], wt[:, k, :], sa, k == 0, k == kc - 1, [wk, sk], [pmk])
                act(mb(m), pm[:], AF.Copy, [pmk], mbk(m))
                act(sq[:, m, :], pm[:], AF.Square, [pmk], [("sq", m)])
            rms_stats(onesmean, "onesmean", 8, [("sq", c) for c in range(8)])
            for m in range(8):
                t = tmp_t[m % 2]
                tk = ("tmp", m % 2)
                dve(lambda e, t=t, m=m: e.tensor_tensor(out=t[:], in0=mb(m), in1=rstd_t[:], op=ALU.mult),
                    mbk(m) + ["rstd"], [tk])
                dve(lambda e, t=t, m=m, tb=tb: e.scalar_tensor_tensor(out=XTa(m, tb), in0=t[:], scalar=modd[:, l, ggi, m, g:g + 1],
                                                                       in1=XTa(m, tb), op0=ALU.mult, op1=ALU.add),
                    [tk, XTk(m, tb), ("modd", l, ggi)], [XTk(m, tb)])
            if after_tb is not None:
                after_tb(tb)

    def ffn(l, g):
        wi = w_ffn_in[l]
        do_mod = ("mod" in parts) and g == groups[0] and l + 1 < depth
        for j in range(22):
            if do_mod and 4 <= j < 16:
                mod_tile(l + 1, j - 4)
                if j - 4 == 11:
                    mod_finish(l + 1)
            wt, wk = load_w([wi[:, j * 128:(j + 1) * 128], wi[:, DFF + j * 128:DFF + (j + 1) * 128]], 8)
            for tb in range(2):
                pa, pak = nextps()
                pu, puk = nextps()
                for k in range(8):
                    mm(pa[:], wt[:, k, 0:128], Ha(k, tb), k == 0, k == 7, [wk, Hk(k, tb)], [pak])
                for k in range(8):
                    mm(pu[:], wt[:, k, 128:256], Ha(k, tb), k == 0, k == 7, [wk, Hk(k, tb)], [puk])
                t = tmp_t[tb]
                tk = ("tmp", tb)
                act(t[:], pa[:], AF.Silu, [pak], [tk])
                dve(lambda e, t=t, pu=pu, j=j, tb=tb: e.tensor_tensor(out=Ua(j, tb), in0=t[:], in1=pu[:], op=ALU.mult),
                    [tk, puk], [Uk(j, tb)])
        outproj_postnorm(l, g, w_ffn_out[l], 22, lambda k, tb: (Ua(k, tb), Uk(k, tb)), 5)

    def load_group(g):
        for tt in range(8):
            s = tt % 2
            dma_sp(pT_f[:, s, :], xin[g, tt * 128:(tt + 1) * 128, :], [], [("pT", 4 * s + i) for i in range(4)], ("xst", s))
            for hb in range(2):
                pb, pbk = nextps()
                for ci in range(4):
                    c = hb * 4 + ci
                    P.op("pe", lambda e, pb=pb, ci=ci, s=s, c=c: e.transpose(out=pb[:, ci * 128:(ci + 1) * 128], in_=pT_f[:, s, c * 128:(c + 1) * 128], identity=ident_f[:]),
                         reads=[("pT", 4 * s + i) for i in range(4)] + ["identf"], writes=[pbk])
                tb = tt // 4
                dstv = XT[:, hb * 4:hb * 4 + 4, tt * 128:(tt + 1) * 128]
                act(dstv, pb[:].rearrange("p (c t) -> p c t", t=128), AF.Copy, [pbk], [XTk(hb * 4 + ci, tb) for ci in range(4)])

    def store_group(g):
        for tt in range(8):
            s = tt % 2
            tb = tt // 4
            for hb in range(2):
                pb, pbk = nextps()
                for ci in range(4):
                    c = hb * 4 + ci
                    P.op("pe", lambda e, pb=pb, ci=ci, c=c, tt=tt: e.transpose(out=pb[:, ci * 128:(ci + 1) * 128], in_=XT[:, c, tt * 128:(tt + 1) * 128], identity=ident_f[:]),
                         reads=[XTk(c, tb), "identf"], writes=[pbk])
                act(pT_f[:, s, hb * 512:(hb + 1) * 512], pb[:], AF.Copy, [pbk], [("pT", 4 * s + 2 * hb), ("pT", 4 * s + 2 * hb + 1)])
            dma_sp(yout[g, tt * 128:(tt + 1) * 128, :], pT_f[:, s, :], [("pT", 4 * s + i) for i in range(4)], [], ("yst", s))

    def run_pipeline(tiles, stage1, stage2, look=2, extras=None):
        n = len(tiles)
        for i in range(min(look, n)):
            stage1(tiles[i])
        for i in range(n):
            if i + look < n:
                stage1(tiles[i + look])
            stage2(tiles[i])
            if extras and i in extras:
                extras[i]()

    def even_mixer(l, g):
        e_i = l // 2
        wie = w_in_even[e_i]
        cv = Carver()
        qT = cv.take(2048, BF16)
        kTz = [cv.take(2048, BF16) for _ in range(2)]
        gsT = cv.take(2048, BF16)
        vtz = [cv.take(2048, BF16).rearrange("p (t d) -> p t d", d=128) for _ in range(2)]
        w2 = cv.take(2 * W2W * 2, BF16).rearrange("p (h x) -> p h x", x=W2W)
        w2tmp = cv.take(W2W * 2, BF16)
        osq = cv.take(1024, BF16)
        if g == 0:
            ktm = cv.take(2048, BF16).rearrange("p (t d) -> p t d", d=128)
            kdec = cv.take(2048, BF16).rearrange("p (t d) -> p t d", d=128)
            kvst1 = cv.take(4096, F32).rearrange("p (t d) -> p t d", d=256)
            retst = [cv.take(2048, F32) for _ in range(2)]
            decP = cv.take(128, F32).rearrange("p (d t h) -> p d t h", d=2, t=2)
        else:
            qraw = cv.take(2048, BF16)
            kraw = cv.take(2048, BF16)
            bias_t = [cv.take(2048, F32) for _ in range(2)]
            stmp = [cv.take(2048, F32) for _ in range(2)]
            decS = cv.take(4096, BF16).rearrange("p (d n) -> p d n", n=1024)
            pst = [cv.take(1024, BF16) for _ in range(2)]
            s0z = cv.take(2048, BF16).rearrange("p (d c v) -> p d c v", d=2, c=4)
            ctxK = cv.take(2048, BF16).rearrange("p (t d) -> p t d", d=512)
            ctxV = cv.take(2048, BF16).rearrange("p (t d) -> p t d", d=512)
            ctxKTz = [cv.take(2048, BF16).rearrange("p (c k) -> p c k", k=256) for _ in range(2)]
            ctxVz = [cv.take(2048, BF16).rearrange("p (t c d) -> p t c d", t=2, c=4) for _ in range(2)]
        cos_t = U_f[:, 4096:5120]
        sin_t = U_f[:, 5120:6144]
        ef = U_f[:, 6144:7168]
        UK_C = [("U", s) for s in range(16, 28)]
        dpp = U_f[:, 7168:7168 + W2W]
        dnn = U_f[:, 9088:9088 + W2W]
        UK_T = [("U", s) for s in range(28, 44)]
        dve(lambda e: e.memset(epoch_t[:], 0.0), [], ["arena_epoch"])
        dma_sp(dpp, hc["dpp"], [], UK_T, ("c", "dpp"))
        dma_sp(dnn, hc["dnn"], [], UK_T, ("c", "dnn"))
        dve(lambda e: e.memset(kTz[0][64:128, :], 0.0), [], [("kT", 0), ("kT", 1)])
        dve(lambda e: e.memset(kTz[1][0:64, :], 0.0), [("kT", 0), ("kT", 1)], [("kT", 0), ("kT", 1)])
        vk_all = [("vtm", tt) for tt in range(8)]
        dve(lambda e: e.memset(vtz[0][:, :, 64:128], 0.0), [], vk_all)
        dve(lambda e: e.memset(vtz[1][:, :, 0:64], 0.0), vk_all, vk_all)
        if g == 1:
            dma_sp(cos_t, hc["cos"], [], UK_C, ("c", "cos"))
            dma_sp(sin_t, hc["sin"], [], UK_C, ("c", "sin"))
            dma_sp(ef, hc["ef"], [], UK_C, ("c", "ef"))
            for t_ in range(2):
                dma_pool(ctxK[:, t_, :].rearrange("p (h d) -> p h d", d=64), ckv[e_i, 0][:, t_ * 128:(t_ + 1) * 128, :].rearrange("h p d -> p h d"), ["arena_epoch"], ["ctxK"], ("c", "ctxK"))
                dma_pool(ctxV[:, t_, :].rearrange("p (h d) -> p h d", d=64), ckv[e_i, 1][:, t_ * 128:(t_ + 1) * 128, :].rearrange("h p d -> p h d"), ["arena_epoch"], ["ctxV"], ("c", "ctxV"))
            for cp in range(4):
                for kt in range(2):
                    P.op("pe", lambda e, cp=cp, kt=kt: e.transpose(out=psb[:, (cp * 2 + kt) * 128:(cp * 2 + kt + 1) * 128], in_=ctxK[:, kt, cp * 128:(cp + 1) * 128], identity=ident_b[:]),
                         reads=["ctxK", "identb"], writes=["psb"])
            psbv = psb[:].rearrange("p (c k) -> p c k", k=256)
            dve(lambda e: e.memset(ctxKTz[0][64:128], 0.0), [], ["ctxKT"])
            dve(lambda e: e.memset(ctxKTz[1][0:64], 0.0), ["ctxKT"], ["ctxKT"])
            act(ctxKTz[0][0:64], psbv[0:64], AF.Copy, ["psb", "ctxKT"], ["ctxKT"])
            act(ctxKTz[1][64:128], psbv[64:128], AF.Copy, ["psb", "ctxKT"], ["ctxKT"])
            cvv = ctxV.rearrange("p t (c d) -> p t c d", d=128)
            dve(lambda e: e.memset(ctxVz[0][:, :, :, 64:128], 0.0), [], ["ctxVz"])
            dve(lambda e: e.memset(ctxVz[1][:, :, :, 0:64], 0.0), ["ctxVz"], ["ctxVz"])
            dve(lambda e: e.tensor_copy(out=ctxVz[0][:, :, :, 0:64], in_=cvv[:, :, :, 0:64]), ["ctxV", "ctxVz"], ["ctxVz"])
            dve(lambda e: e.tensor_copy(out=ctxVz[1][:, :, :, 64:128], in_=cvv[:, :, :, 64:128]), ["ctxV", "ctxVz"], ["ctxVz"])
            dve(lambda e: e.memset(s0z, 0.0), [], ["s0"])
            for dr in range(2):
                for hh in range(2):
                    dma_pool(s0z[64 * hh:64 * hh + 64, dr, :, 64 * hh:64 * hh + 64], sret[e_i, dr].rearrange("(c two) k v -> two k c v", two=2)[hh], ["s0", "arena_epoch"], ["s0"], ("c", "s0"))
        else:
            for dr in range(2):
                for h in range(8):
                    col = e_i * 16 + dr * 8 + h
                    dve(lambda e, dr=dr, h=h, col=col: e.tensor_scalar(decP[:, dr, :, h], epd[:, dr, :], lg[:, col:col + 1], None, op0=ALU.mult), ["epd", "lg"], ["decP"])
            act(decP, decP, AF.Exp, ["decP", "eps"], ["decP"], bias=nln8[:, 0:1])
        PSOK = [("pso", 0), ("pso", 1)]
        PSDK = [("psd", 0), ("psd", 1)]

        def proj_fm(wt, wk, col, dst, dstk, func=AF.Copy):
            for tb in range(2):
                pp, ppk = nextps()
                for k in range(8):
                    mm(pp[:], wt[:, k, col:col + 128], Ha(k, tb), k == 0, k == 7, [wk, Hk(k, tb)], [ppk])
                act(dst[:, tb * 512:(tb + 1) * 512], pp[:], func, [ppk], [(dstk, tb)])

        def proj_k(wt, wk, col):
            for tb in range(2):
                sl = slice(tb * 512, (tb + 1) * 512)
                pp, ppk = nextps()
                for k in range(8):
                    mm(pp[:], wt[:, k, col:col + 128], Ha(k, tb), k == 0, k == 7, [wk, Hk(k, tb)], [ppk])
                act(kTz[0][0:64, sl], pp[0:64, :], AF.Copy, [ppk], [("kT", tb)])
                act(kTz[1][64:128, sl], pp[64:128, :], AF.Copy, [ppk, ("kT", tb)], [("kT", tb)])

        def rope(raw, rawk, dsts, dstk):
            for tb in range(2):
                pp, ppk = nextps()
                sl = slice(tb * 512, (tb + 1) * 512)
                mm(pp[:], rotT[:], raw[:, sl], True, True, ["rotT", (rawk, tb)], [ppk])
                t0, t1 = tmp_t
                dve(lambda e, sl=sl: e.tensor_tensor(out=t0[:], in0=raw[:, sl], in1=cos_t[:, sl], op=ALU.mult), [(rawk, tb)] + UK_C, [("tmp", 0)])
                dve(lambda e, sl=sl, pp=pp: e.tensor_tensor(out=t1[:], in0=pp[:], in1=sin_t[:, sl], op=ALU.mult), [ppk] + UK_C, [("tmp", 1)])
                if len(dsts) == 1:
                    dve(lambda e, sl=sl: e.tensor_tensor(out=dsts[0][:, sl], in0=t0[:], in1=t1[:], op=ALU.add), [("tmp", 0), ("tmp", 1)], [(dstk, tb)])
                else:
                    for hh in range(2):
                        pr = slice(64 * hh, 64 * hh + 64)
                        dve(lambda e, sl=sl, pr=pr, hh=hh: e.tensor_tensor(out=dsts[hh][pr, sl], in0=t0[pr, :], in1=t1[pr, :], op=ALU.add), [("tmp", 0), ("tmp", 1), (dstk, tb)], [(dstk, tb)])

        def proj_tm(wt, wk, want_k):
            for tt in range(8):
                tb = tt // 4
                pp, ppk = nextps()
                for k in range(8):
                    mm(pp[:, 0:256], H_bf[:, k, tt * 128:(tt + 1) * 128], wt[:, k, 128:384], k == 0, k == 7, [wk, Hk(k, tb)], [ppk])
                act(vtz[0][:, tt, 0:64], pp[:, 128:192], AF.Copy, [ppk], [("vtm", tt)])
                act(vtz[1][:, tt, 64:128], pp[:, 192:256], AF.Copy, [ppk, ("vtm", tt)], [("vtm", tt)])
                yield tt, pp, ppk

        _bi = [0]
        _pi = [0]

        def npi():
            pi = _pi[0] % 8
            _pi[0] += 1
            return pi

        def mk_tiles():
            tl = []
            for tb in range(2):
                if g == 0:
                    for bl in range(2):
                        for hh in range(2):
                            tl.append(dict(tb=tb, hh=hh, bl=bl, last=(bl == 1 and hh == 1)))
                else:
                    for hh in range(2):
                        for kti in range(8):
                            tl.append(dict(tb=tb, hh=hh, kti=kti, last=(hh == 1 and kti == 7)))
            return tl

        for cp in range(4):
            wt, wk = load_w([wie[:, cp * 128:(cp + 1) * 128], wie[:, 512 + cp * 128:512 + (cp + 1) * 128],
                             wie[:, 1024 + cp * 128:1024 + (cp + 1) * 128]], 8)
            proj_fm(wt, wk, 0, qT, "qT")
            proj_k(wt, wk, 128)
            for tt, pp, ppk in proj_tm(wt, wk, True):
                if g == 0:
                    dve(lambda e, pp=pp, tt=tt: e.tensor_copy(out=kvst1[:, tt % 4, :], in_=pp[:, 0:256]), [ppk], [("kvst", 0)])
                    if tt % 4 == 3 and "noKV" not in parts:
                        sI = tt // 4
                        for bl in range(2):
                            b = 2 * sI + bl
                            for kvi in range(2):
                                for hh_ in range(2):
                                    src = kvst1[:, 2 * bl:2 * bl + 2, kvi * 128 + hh_ * 64:kvi * 128 + hh_ * 64 + 64]
                                    dst = nkv[b, e_i, kvi, 2 * cp + hh_].rearrange("(t p) d -> p t d", p=128)
                                    dma_sp(dst, src, [("kvst", 0)], [], ("kvst", 0))
            tilesA = mk_tiles()

            def a_stage1(t):
                tb, hh = t["tb"], t["hh"]
                h = 2 * cp + hh
                pS, pSk = nextps()
                pi = npi()
                t["pi"] = pi
                if g == 0:
                    b = 2 * tb + t["bl"]
                    qs = slice(b * 256, (b + 1) * 256)
                    for kt in range(2):
                        ks = slice((2 * b + kt) * 128, (2 * b + kt + 1) * 128)
                        mm(pS[:, kt * 256:(kt + 1) * 256], kTz[hh][:, ks], qT[:, qs], True, True, [("kT", tb), ("qT", tb)], [pSk])
                    act(pTt[:, pi, :], pS[:], AF.Exp, [pSk], [("pT", pi)], scale=0.125)
                else:
                    hf = tb
                    kti = t["kti"]
                    qs = slice(hf * 512, (hf + 1) * 512)
                    if kti < 6:
                        tt = 2 * hf + kti
                        ks = slice(tt * 128, (tt + 1) * 128)
                        mm(pS[:], kTz[hh][:, ks], qT[:, qs], True, True, [("kT", tt // 4), ("qT", tb)], [pSk])
                        bi = _bi[0] % 2
                        _bi[0] += 1
                        dma_sp(bias_t[bi], nab[e_i, h, hf, kti], ["arena_epoch"], [("bias", bi)], ("bias", bi))
                        stt = stmp[bi]
                        dve(lambda e, stt=stt, pS=pS, bi=bi: e.scalar_tensor_tensor(out=stt, in0=pS[:], scalar=0.125, in1=bias_t[bi], op0=ALU.mult, op1=ALU.add),
                            [pSk, ("bias", bi)], [("stmp", bi)])
                        act(pTt[:, pi, :], stt, AF.Exp, [("stmp", bi)], [("pT", pi)])
                    else:
                        kt = kti - 6
                        mm(pS[:], ctxKTz[hh][:, cp, kt * 128:(kt + 1) * 128], qT[:, qs], True, True, ["ctxKT", ("qT", tb)], [pSk])
                        act(pTt[:, pi, :], pS[:], AF.Exp, [pSk], [("pT", pi)], scale=0.125)

            def a_stage2(t):
                tb, hh, pi = t["tb"], t["hh"], t["pi"]
                if g == 0:
                    bl = t["bl"]
                    b = 2 * tb + bl
                    for kt in range(2):
                        tt = 2 * b + kt
                        st_ = (hh == 0 and kt == 0)
                        sp_ = (hh == 1 and kt == 1)
                        mm(PSO[:, bl * 256:(bl + 1) * 256], vtz[hh][:, tt, :], pTt[:, pi, kt * 256:(kt + 1) * 256], st_, sp_, [("vtm", tt), ("pT", pi)], PSOK)
                        mm(PSD[:, bl * 256:(bl + 1) * 256], onesz[hh][:], pTt[:, pi, kt * 256:(kt + 1) * 256], st_, sp_, ["onesz", ("pT", pi)], PSDK)
                else:
                    hf = tb
                    kti = t["kti"]
                    if kti < 6:
                        tt = 2 * hf + kti
                        vl = vtz[hh][:, tt, :]
                        vk = ("vtm", tt)
                    else:
                        kt = kti - 6
                        vl = ctxVz[hh][:, kt, cp, :]
                        vk = "ctxVz"
                    st_ = (hh == 0 and kti == 0)
                    sp_ = (hh == 1 and kti == 7)
                    mm(PSO[:, :], vl, pTt[:, pi, :], st_, sp_, [vk, ("pT", pi)], PSOK)
                    mm(PSD[:, :], onesz[hh][:], pTt[:, pi, :], st_, sp_, ["onesz", ("pT", pi)], PSDK)
                if t["last"]:
                    act(rt_t[:], PSD[:], AF.Ln, PSDK, ["rt"])
                    act(rstd_t[:], rt_t[:], AF.Exp, ["rt"], ["rstd"], scale=-1.0)
                    dve(lambda e, cp=cp, tb=tb: e.tensor_tensor(out=Ua(cp, tb), in0=PSO[:], in1=rstd_t[:], op=ALU.mult), PSOK + ["rstd"], [Uk(cp, tb)])

            w2x = {}
            _sl = len(tilesA) // 8
            for hh in range(2):
                h_ = 2 * cp + hh
                cf_ = e_i * 16 + h_
                cb_ = e_i * 16 + 8 + h_
                w2x[(3 * hh + 1) * _sl] = (lambda hh=hh, cf_=cf_: act(w2[:, hh, :], dpp, AF.Exp, UK_T + ["lg"], [("w2", hh)], scale=lg[:, cf_:cf_ + 1]))
                w2x[(3 * hh + 2) * _sl] = (lambda hh=hh, cb_=cb_: act(w2tmp, dnn, AF.Exp, UK_T + ["lg"], ["w2tmp"], scale=lg[:, cb_:cb_ + 1]))
                w2x[(3 * hh + 3) * _sl] = (lambda hh=hh: dve(lambda e, hh=hh: e.tensor_tensor(out=w2[:, hh, :], in0=w2[:, hh, :], in1=w2tmp, op=ALU.add), [("w2", hh), "w2tmp"], [("w2", hh)]))
            run_pipeline(tilesA, a_stage1, a_stage2, extras=w2x)

            wt, wk = load_w([wie[:, 1536 + cp * 128:1536 + (cp + 1) * 128], wie[:, 2048 + cp * 128:2048 + (cp + 1) * 128],
                             wie[:, 2560 + cp * 128:2560 + (cp + 1) * 128], wie[:, 3072 + cp * 128:3072 + (cp + 1) * 128]], 8)
            if g == 1:
                proj_fm(wt, wk, 0, qraw, "qraw")
                proj_fm(wt, wk, 128, kraw, "kraw")
                rope(qraw, "qraw", [qT], "qT")
                rope(kraw, "kraw", kTz, "kT")
            else:
                proj_fm(wt, wk, 0, qT, "qT")
                proj_k(wt, wk, 128)
            proj_fm(wt, wk, 384, gsT, "gsT", AF.Silu)
            for tt, pp, ppk in proj_tm(wt, wk, False):
                if g == 0:
                    dve(lambda e, pp=pp, tt=tt: e.tensor_copy(out=ktm[:, tt, :], in_=pp[:, 0:128]), [ppk], [("ktm", tt)])
            tilesB = mk_tiles()

            def b_stage1(t):
                tb, hh = t["tb"], t["hh"]
                h = 2 * cp + hh
                pr = slice(64 * hh, 64 * hh + 64)
                cf = e_i * 16 + h
                cb = e_i * 16 + 8 + h
                pS, pSk = nextps()
                pi = npi()
                t["pi"] = pi
                if g == 0:
                    b = 2 * tb + t["bl"]
                    qs = slice(b * 256, (b + 1) * 256)
                    for kt in range(2):
                        ks = slice((2 * b + kt) * 128, (2 * b + kt + 1) * 128)
                        mm(pS[:, kt * 256:(kt + 1) * 256], kTz[hh][:, ks], qT[:, qs], True, True, [("kT", tb), ("qT", tb)], [pSk])
                    for kt in range(2):
                        s0x = XC - 128 * kt
                        dve(lambda e, pS=pS, pi=pi, kt=kt, s0x=s0x, hh=hh: e.scalar_tensor_tensor(
                            out=pTt[:, pi, kt * 256:(kt + 1) * 256], in0=pS[:, kt * 256:(kt + 1) * 256], scalar=0.125,
                            in1=w2[:, hh, s0x:s0x + 256], op0=ALU.mult, op1=ALU.mult), [pSk, ("w2", hh)], [("pT", pi)])
                else:
                    hf = tb
                    kti = t["kti"]
                    qs = slice(hf * 512, (hf + 1) * 512)
                    if hf == 0 and kti == 0:
                        act(decS[pr, 0, :], ef[pr, :], AF.Exp, UK_C + ["lg"], [("decS", hh)], scale=lg[pr, cf:cf + 1])
                        act(decS[pr, 1, :], ef[pr, :], AF.Exp, UK_C + ["lg", "nlg", "lg1025"], [("decS", hh)], scale=nlg[pr, cb:cb + 1], bias=lg1025[pr, cb:cb + 1])
                    tt = kti
                    ks = slice(tt * 128, (tt + 1) * 128)
                    mm(pS[:], kTz[hh][:, ks], qT[:, qs], True, True, [("kT", tt // 4), ("qT", tb)], [pSk])
                    s0x = 512 * hf - 128 * tt + XC
                    dve(lambda e, pS=pS, pi=pi, s0x=s0x, hh=hh: e.scalar_tensor_tensor(
                        out=pTt[:, pi, :], in0=pS[:], scalar=0.125, in1=w2[:, hh, s0x:s0x + 512], op0=ALU.mult, op1=ALU.mult),
                        [pSk, ("w2", hh)], [("pT", pi)])
                    if kti == 7:
                        for dr in range(2):
                            dve(lambda e, dr=dr, pr=pr, qs=qs: e.tensor_tensor(out=pst[dr][pr, :], in0=qT[pr, qs], in1=decS[pr, dr, qs], op=ALU.mult),
                                [("qT", tb), ("decS", hh), ("pst", dr)], [("pst", dr)])

            def b_stage2(t):
                tb, hh, pi = t["tb"], t["hh"], t["pi"]
                if g == 0:
                    bl = t["bl"]
                    b = 2 * tb + bl
                    for kt in range(2):
                        tt = 2 * b + kt
                        mm(PSO[:, bl * 256:(bl + 1) * 256], vtz[hh][:, tt, :], pTt[:, pi, kt * 256:(kt + 1) * 256], hh == 0 and kt == 0, hh == 1 and kt == 1,
                           [("vtm", tt), ("pT", pi)], PSOK)
                else:
                    kti = t["kti"]
                    tt = kti
                    mm(PSO[:, :], vtz[hh][:, tt, :], pTt[:, pi, :], hh == 0 and kti == 0, False, [("vtm", tt), ("pT", pi)], PSOK)
                    if t["last"]:
                        for dr in range(2):
                            mm(PSO[:, :], s0z[:, dr, cp, :], pst[dr][:, :], False, dr == 1, ["s0", ("pst", dr)], PSOK)
                if t["last"]:
                    act(osq, PSO[:], AF.Square, PSOK, ["osq"])
                    pa, pak = nextps()
                    mm(pa[:], bd64[:], osq, True, True, ["bd64", "osq"], [pak])
                    act(rt_t[:], pa[:], AF.Ln, [pak, "eps"], ["rt"], bias=eps_t[:, 0:1])
                    act(rstd_t[:], rt_t[:], AF.Exp, ["rt"], ["rstd"], scale=-0.5)
                    t0 = tmp_t[0]
                    dve(lambda e: e.tensor_tensor(out=t0[:], in0=PSO[:], in1=rstd_t[:], op=ALU.mult), PSOK + ["rstd"], [("tmp", 0)])
                    dve(lambda e, cp=cp, tb=tb: e.tensor_tensor(out=Ua(4 + cp, tb), in0=t0[:], in1=gsT[:, tb * 512:(tb + 1) * 512], op=ALU.mult),
                        [("tmp", 0), ("gsT", tb)], [Uk(4 + cp, tb)])

            run_pipeline(tilesB, b_stage1, b_stage2)
            if g == 0 and "noS" not in parts:
                for b in range(4):
                    ri = b % 2
                    pR, pRk = nextps()
                    for dr in range(2):
                        dve(lambda e, b=b, dr=dr, cp=cp: e.tensor_tensor(
                            out=kdec[:, 2 * b:2 * b + 2, :].rearrange("p t (h d) -> p t h d", d=64),
                            in0=ktm[:, 2 * b:2 * b + 2, :].rearrange("p t (h d) -> p t h d", d=64),
                            in1=decP[:, dr, :, 2 * cp:2 * cp + 2].unsqueeze(3).to_broadcast([128, 2, 2, 64]), op=ALU.mult),
                            [("ktm", 2 * b), ("ktm", 2 * b + 1), "decP"], [("kdec", b)])
                        for hh in range(2):
                            col = (dr * 2 + hh) * 64
                            for kt in range(2):
                                tt = 2 * b + kt
                                mm(pR[0:64, col:col + 64], kdec[:, tt, 64 * hh:64 * hh + 64], vtz[hh][:, tt, 64 * hh:64 * hh + 64], kt == 0, kt == 1,
                                   [("kdec", b), ("vtm", tt)], [pRk])
                    act(retst[ri][0:64, 0:256], pR[0:64, 0:256], AF.Copy, [pRk], [("retst", ri)])
                    for r_ in range(2):
                        dst = nret[b, e_i, r_, 2 * cp:2 * cp + 2].rearrange("h k v -> k h v")
                        dma_sp(dst, retst[ri][0:64, r_ * 128:(r_ + 1) * 128].rearrange("p (h v) -> p h v", h=2), [("retst", ri)], [], ("retst", ri))

        if g == 0:
            dve(lambda e: e.memset(kvst1[:, 0, 0:1], 0.0), [], [("kvst", 0), ("retst", 0), ("retst", 1)])
        if "noO" not in parts:
            outproj_postnorm(l, g, w_out_even[e_i], 8, lambda k, tb: (Ua(k, tb), Uk(k, tb)), 2, in_u=True, after_tb=POST_HOOK[0])

    def odd_mixer(l, g):
        o_i = l // 2
        wio = w_in_odd[o_i]
        cv = Carver()
        f4 = lambda: cv.take(4096, F32)
        b2 = lambda: cv.take(2048, BF16)
        sig2 = [f4(), f4()]
        fT, rA, R, qs_ = f4(), f4(), f4(), f4()
        A2 = [f4(), f4()]
        qe2 = [b2(), b2()]
        ke2 = [b2(), b2()]
        kdtm2 = [cv.take(2048, BF16).rearrange("p (t d) -> p t d", d=128) for _ in range(2)]
        gsTS = [b2(), b2()]
        vtmS = [cv.take(2048, BF16).rearrange("p (t d) -> p t d", d=128) for _ in range(2)]
        attm2 = [cv.take(512, BF16).rearrange("p (t s) -> p t s", s=32) for _ in range(2)]
        Sst = [[cv.take(512, F32) for _ in range(2)] for _ in range(2)]
        NR = 4
        Sbf = [[cv.take(256, BF16) for _ in range(NR)] for _ in range(2)]
        hgst = [cv.take(512, F32) for _ in range(2)]
        oacc = Uu[:, 8192:8192 + 2048].bitcast(F32)
        kdT = Uu[:, 10240:10240 + 1024]
        osq = Uu[:, 11264:11264 + 512]
        C = 32
        nseq = 4 if g == 0 else 1
        T = 256 if g == 0 else 1024
        ncs = T // C
        NST = 1024 // C
        dve(lambda e: e.memset(epoch_t[:], 0.0), [], ["arena_epoch"])
        Rb = R.bitcast(BF16)
        Rm = [Rb[:, 0:1024], Rb[:, 1024:2048]]
        dve(lambda e: e.memset(Rb, 1.0), [], ["R"])
        dve(lambda e: e.memset(Rm[0].rearrange("p (c t) -> p c t", t=C)[:, :, 0:1], 0.0), ["R"], ["R"])
        dve(lambda e: e.memset(Rm[1].rearrange("p (c t) -> p c t", t=C)[:, :, C - 1:C], 0.0), ["R"], ["R"])
        _hg = [0]

        def ps3():
            i = _rr[0] % 3
            _rr[0] += 1
            return ps[i], ("ps", i)

        def load_slot(i, pieces, kc=8):
            ntot = sum(p.shape[1] for p in pieces)
            view = wbuf[i][:, 0:kc * ntot].rearrange("p (k n) -> p k n", n=ntot)
            off = 0
            for p in pieces:
                n = p.shape[1]
                dma_pool(view[:, :, off:off + n], p.rearrange("(k p) n -> p k n", p=128), [], [("wb", i)], ("wb", i))
                off += n
            return view, ("wb", i)

        Wt = {}

        def load_head(h):
            wt, wk = load_slot(h % 2, [wio[:, h * 128:(h + 1) * 128], wio[:, 1024 + h * 128:1024 + (h + 1) * 128], wio[:, 2048 + h * 128:2048 + (h + 1) * 128]])
            wt2, wk2 = load_slot(2, [wio[:, 3072 + h * 128:3072 + (h + 1) * 128], wio[:, 4096 + h * 128:4096 + (h + 1) * 128]])
            Wt[h] = (wt, wk, wt2, wk2)

        def fm_thunks(h, which):
            wt, wk, wt2, wk2 = Wt[h]
            out = []
            for tb in range(2):
                def th(tb=tb):
                    sl = slice(tb * 512, (tb + 1) * 512)
                    pp, ppk = ps3()
                    if which == "q":
                        w_, wk_, col, dst, func, key = wt, wk, 0, qs_, AF.Silu, ("qs", tb)
                    elif which == "g":
                        w_, wk_, col, dst, func, key = wt2, wk2, 128, gsTS[h % 2], AF.Silu, ("gsT", h % 2, tb)
                    else:
                        dr = which
                        w_, wk_, col, dst, func, key = wt, wk, 128 * (1 + dr), sig2[dr], AF.Sigmoid, ("sig", dr, tb)
                    for k in range(8):
                        mm(pp[:], w_[:, k, col:col + 128], Ha(k, tb), k == 0, k == 7, [wk_, Hk(k, tb)], [ppk])
                    act(dst[:, sl], pp[:], func, [ppk], [key])
                out.append(th)
            return out

        def v_thunks(h):
            wt, wk, wt2, wk2 = Wt[h]
            vtm = vtmS[h % 2]
            st_ = {}
            out = []
            for tt in range(8):
                def th(tt=tt):
                    tb = tt // 4
                    if tt % 4 == 0:
                        st_["pp"] = ps3()
                    pp, ppk = st_["pp"]
                    for k in range(8):
                        mm(pp[:, (tt % 4) * 128:(tt % 4 + 1) * 128], H_bf[:, k, tt * 128:(tt + 1) * 128], wt2[:, k, 0:128], k == 0, k == 7, [wk2, Hk(k, tb)], [ppk])
                    if tt % 4 == 3:
                        act(vtm[:, tt - 3:tt + 1, :], pp[:].rearrange("p (t d) -> p t d", d=128), AF.Copy, [ppk], [("vtm", h % 2, tb)])
                out.append(th)
            return out

        def chain_ops(h, dr):
            A, qe, ke = A2[dr], qe2[dr], ke2[dr]
            sig = sig2[dr]
            Ak, qek, kek = ("A", dr), ("qe", dr), ("ke", dr)
            lo = lowt[:, o_i, dr, h:h + 1]
            om = omlt[:, o_i, dr, h:h + 1]
            nom = nomlt[:, o_i, dr, h:h + 1]
            sgk = [("sig", dr, 0), ("sig", dr, 1)]
            Av = A.rearrange("p (c t) -> p c t", t=C)
            alast = Av[:, :, C - 1:C] if dr == 0 else Av[:, :, 0:1]
            ops = []
            ops.append(lambda: dve(lambda e: e.tensor_scalar(fT, sig, om, lo, op0=ALU.mult, op1=ALU.add), sgk + ["omlt", "lowt"], ["fT"]))
            ops.append(lambda: dve(lambda e: e.tensor_scalar(sig, sig, nom, om, op0=ALU.mult, op1=ALU.add), sgk + ["omlt", "nomlt"], sgk))
            ops.append(lambda: act(fT, fT, AF.Ln, ["fT"], ["fT"]))
            if dr == 0:
                ops.append(lambda: dve(lambda e: e.tensor_tensor_scan(out=rA, data0=Rm[0], data1=fT, initial=0.0, op0=ALU.mult, op1=ALU.add), ["fT", "R"], ["rA"]))
            else:
                ops.append(lambda: dve(lambda e: e.tensor_tensor_scan(out=rA[:, ::-1], data0=Rm[1][:, ::-1], data1=fT[:, ::-1], initial=0.0, op0=ALU.mult, op1=ALU.add), ["fT", "R"], ["rA"]))
            ops.append(lambda: act(A, rA, AF.Exp, ["rA"], [Ak]))
            ops.append(lambda: dve(lambda e: e.tensor_scalar_max(rA, rA, -69.0), ["rA"], ["rA"]))
            ops.append(lambda: act(rA, rA, AF.Exp, ["rA"], ["rA"], scale=-1.0))
            ops.append(lambda: dve(lambda e: e.scalar_tensor_tensor(out=qe, in0=qs_, scalar=128.0 ** -0.5, in1=A, op0=ALU.mult, op1=ALU.mult), [("qs", 0), ("qs", 1), Ak], [qek]))
            ops.append(lambda: dve(lambda e: e.tensor_tensor(out=sig, in0=sig, in1=rA, op=ALU.mult), sgk + ["rA"], sgk))
            ops.append(lambda: dve(lambda e: e.tensor_copy(out=ke, in_=sig), sgk, [kek]))
            ops.append(lambda: dve(lambda e: e.tensor_tensor(out=kdT.rearrange("p (c t) -> p c t", t=C), in0=sig.rearrange("p (c t) -> p c t", t=C),
                                                             in1=alast.to_broadcast([128, 1024 // C, C]), op=ALU.mult), sgk + [Ak], ["kdT"]))
            return ops

        def post_chain(h, dr):
            qe, ke, kdtm, attm = qe2[dr], ke2[dr], kdtm2[dr], attm2[dr]
            qek, kek, kdk, atk = ("qe", dr), ("ke", dr), ("kdtm", dr), ("attm", dr)
            for tt in range(8):
                P.op("pe", lambda e, tt=tt: e.transpose(out=psb[:, tt * 128:(tt + 1) * 128], in_=kdT[:, tt * 128:(tt + 1) * 128], identity=ident_b[:]),
                     reads=["kdT", "identb"], writes=["psb"])
            act(kdtm, psb[:].rearrange("p (t d) -> p t d", d=128), AF.Copy, ["psb"], [kdk])
            pA, pAk = ps3()
            for c in range(NST):
                tt = c // 4
                q4 = 32 * (c % 4)
                cs = slice(c * C, (c + 1) * C)
                mm(pA[q4:q4 + 32, tt * 32:(tt + 1) * 32], ke[:, cs], qe[:, cs], True, True, [kek, qek], [pAk], tp=(0, q4))
            dve(lambda e, pA=pA, dr=dr, attm=attm: e.tensor_tensor(out=attm, in0=pA[:, 0:256].rearrange("p (t s) -> p t s", s=32),
                                                                      in1=hmask[:, dr, :].unsqueeze(1).to_broadcast([128, 8, 32]), op=ALU.mult),
                [pAk, "hmask"], [atk])

        def interleave(chain, hoisted, positions):
            j = 0
            for ci, op_ in enumerate(chain):
                op_()
                while j < len(hoisted) and positions[j] <= ci:
                    hoisted[j]()
                    j += 1
            while j < len(hoisted):
                hoisted[j]()
                j += 1

        load_head(0)
        for th in fm_thunks(0, "q") + fm_thunks(0, "g") + v_thunks(0) + fm_thunks(0, 0):
            th()
        for h in range(8):
            nxt = h + 1 < 8
            if nxt:
                load_head(h + 1)
            hoistA = fm_thunks(h, 1) + ((fm_thunks(h + 1, "g") + v_thunks(h + 1)) if nxt else [])
            interleave(chain_ops(h, 0), hoistA, [0, 0] + [1 + (i * 9) // 10 for i in range(10)])
            post_chain(h, 0)
            hoistB = (fm_thunks(h + 1, 0) + fm_thunks(h + 1, "q")) if nxt else []
            interleave(chain_ops(h, 1), hoistB, [0, 1, 7, 8])
            post_chain(h, 1)

            def chunk_of(dr, i):
                b, ci = divmod(i, ncs)
                return b * ncs + (ci if dr == 0 else ncs - 1 - ci)

            psb_f = psb[:].bitcast(F32)
            ubanks = [(ps[0][:, 0:128], ("ps", 0)), (ps[1][:, 0:128], ("ps", 1)), (ps[2][:, 0:128], ("ps", 2)), (psb_f[:, 0:128], "psb")]

            def emit_U(i):
                out = []
                for dr in range(2):
                    c = chunk_of(dr, i)
                    tt = c // 4
                    q4 = 32 * (c % 4)
                    pU, pUk = ubanks[(2 * i + dr) % 4]
                    mm(pU, kdtm2[dr][q4:q4 + 32, tt, :], vtmS[h % 2][q4:q4 + 32, tt, :], True, True, [("kdtm", dr), ("vtm", h % 2, tt // 4)], [pUk], tp=(q4, 0))
                    out.append((pU, pUk))
                return out

            if g == 1:
                for dr in range(2):
                    dma_sp(Sst[dr][1], shg[o_i, dr, h], ["arena_epoch"], [("S", dr, 1)], ("S", dr))
                    act(Sbf[dr][NR - 1], Sst[dr][1], AF.Copy, [("S", dr, 1)], [("Sbf", dr, NR - 1)])
            LOOK = 1
            dve(lambda e: e.memset(oacc, 0.0), [("oacc", j) for j in range(4)], [("oacc", j) for j in range(4)])
            pend = {}
            for i in range(min(LOOK, NST)):
                pend[i] = emit_U(i)
            for i in range(NST):
                if i + LOOK < NST:
                    pend[i + LOOK] = emit_U(i + LOOK)
                b, ci = divmod(i, ncs)
                first = (ci == 0) and g == 0
                for dr in range(2):
                    c = chunk_of(dr, i)
                    tt = c // 4
                    q4 = 32 * (c % 4)
                    cs = slice(c * C, (c + 1) * C)
                    bank = (3 if dr == 0 else 5) + ((i // 8) % 2)
                    pO, pOk = ps[bank], ("ps", bank)
                    oc = slice((i % 8) * C, (i % 8 + 1) * C)
                    mm(pO[:, oc], vtmS[h % 2][q4:q4 + 32, tt, :], attm2[dr][q4:q4 + 32, tt, :], True, first, [("vtm", h % 2, tt // 4), ("attm", dr)], [pOk], tp=(q4, 0))
                    if not first:
                        mm(pO[:, oc], Sbf[dr][(i - 1) % NR], qe2[dr][:, cs], False, True, [("Sbf", dr, (i - 1) % NR), ("qe", dr)], [pOk])
                for dr in range(2):
                    c = chunk_of(dr, i)
                    pU, pUk = pend[i][dr]
                    A = A2[dr]
                    al = A[:, c * C + C - 1:c * C + C] if dr == 0 else A[:, c * C:c * C + 1]
                    cur, prv = Sst[dr][i % 2], Sst[dr][(i - 1) % 2]
                    if first:
                        dve(lambda e, cur=cur, pU=pU: e.tensor_copy(out=cur, in_=pU), [pUk], [("S", dr, i % 2)])
                    else:
                        dve(lambda e, cur=cur, prv=prv, pU=pU, al=al: e.scalar_tensor_tensor(out=cur, in0=prv, scalar=al, in1=pU, op0=ALU.mult, op1=ALU.add),
                            [pUk, ("S", dr, (i - 1) % 2), ("A", dr)], [("S", dr, i % 2)])
                    last = (ci == ncs - 1)
                    if not (last and (g == 0 or i == NST - 1)):
                        act(Sbf[dr][i % NR], cur, AF.Copy, [("S", dr, i % 2)], [("Sbf", dr, i % NR)])
                    if last and g == 0:
                        hi = _hg[0] % 2
                        _hg[0] += 1
                        act(hgst[hi], cur, AF.Copy, [("S", dr, i % 2)], [("hgst", hi)])
                        dma_sp(nhg[b, o_i, dr, h], hgst[hi], [("hgst", hi)], [], ("hgst", hi))
                del pend[i]
                if i % 8 == 7:
                    for dr in range(2):
                        bank = (3 if dr == 0 else 5) + ((i // 8) % 2)
                        pO, pOk = ps[bank], ("ps", bank)
                        c_first = chunk_of(dr, i - 7)
                        c_last = chunk_of(dr, i)
                        lo_c = min(c_first, c_last)
                        osl = slice(lo_c * C, (lo_c + 8) * C)
                        rng_ = lo_c // 8
                        srcv = pO[:, 0:8 * C].rearrange("p (c t) -> p c t", t=C)
                        if dr == 1:
                            srcv = srcv[:, ::-1, :]
                        if True:
                            dve(lambda e, osl=osl, srcv=srcv: e.tensor_tensor(out=oacc[:, osl].rearrange("p (c t) -> p c t", t=C), in0=oacc[:, osl].rearrange("p (c t) -> p c t", t=C),
                                                                             in1=srcv, op=ALU.add), [pOk, ("oacc", lo_c // 8)], [("oacc", lo_c // 8)])
            for tb in range(2):
                sl = slice(tb * 512, (tb + 1) * 512)
                oak = [("oacc", j) for j in range(4)]
                act(osq, oacc[:, sl], AF.Square, oak, ["osq"])
                pa, pak = ps3()
                mm(pa[:], om128[:], osq, True, True, ["om128", "osq"], [pak])
                act(rt_t[:], pa[:], AF.Ln, [pak, "eps"], ["rt"], bias=eps_t[:, 0:1])
                act(rstd_t[:], rt_t[:], AF.Exp, ["rt"], ["rstd"], scale=-0.5)
                t0 = tmp_t[0]
                dve(lambda e, sl=sl: e.scalar_tensor_tensor(out=t0[:], in0=oacc[:, sl], scalar=gnT[:, o_i:o_i + 1], in1=rstd_t[:], op0=ALU.mult, op1=ALU.mult),
                    oak + ["rstd", "gnT"], [("tmp", 0)])
                dve(lambda e, sl=sl, h=h, tb=tb: e.tensor_tensor(out=Ua(h, tb), in0=t0[:], in1=gsTS[h % 2][:, sl], op=ALU.mult), [("tmp", 0), ("gsT", h % 2, tb)], [Uk(h, tb)])
        if g == 0:
            dve(lambda e: e.memset(hgst[0][:, 0:1], 0.0), [], [("hgst", 0), ("hgst", 1)])
        outproj_postnorm(l, g, w_out_odd[o_i], 8, lambda k, tb: (Ua(k, tb), Uk(k, tb)), 2, in_u=True, after_tb=POST_HOOK[0])

    POST_HOOK = [None]
    for gi_, g in enumerate(groups):
        load_group(g)
        if gi_ == 0 and "mod" in parts:
            for mq in range(12):
                mod_tile(0, mq)
            mod_finish(0)
        for l in range(depth):
            if "pre" in parts:
                prenorm(l, 0, g)
            if "mixer" in parts:
                POST_HOOK[0] = (lambda tb, l=l, g=g: prenorm(l, 1, g, (tb,))) if "pre" in parts else None
                if l % 2 == 0:
                    even_mixer(l, g)
                else:
                    odd_mixer(l, g)
            elif "pre" in parts:
                prenorm(l, 1, g)
            if "ffn" in parts:
                ffn(l, g)
        store_group(g)

    P.emit()
    st.close()
    global LASTP
    LASTP = P
    return nc


_CACHE = {}


def make_in_maps(inputs):
    f = lambda a: np.ascontiguousarray(np.asarray(a, dtype=np.float32))
    hcs = {"c_" + k: v for k, v in _host_consts().items()}
    nabt = _nabias(f(inputs["rpb"]))
    shared = {k: f(inputs[k]) for k in ["w_mod", "b_mod", "norm_g", "w_in_even", "w_out_even", "w_in_odd", "w_out_odd",
                                       "hgrn_lb", "hgrn_gnorm", "w_ffn_in", "w_ffn_out"]}
    shared["ret_decay"] = f(inputs["ret_decay"]).reshape(32)
    shared["nab"] = nabt
    shared.update(hcs)
    xp = f(inputs["x_prompt"])
    xs = f(inputs["x_sample"])
    maps = []
    for i in range(8):
        m = dict(shared)
        m["xin"] = np.ascontiguousarray(np.stack([xp[4 * i:4 * i + 4].reshape(1024, D), xs[i]], axis=0))
        m["ckv"] = f(inputs["cache_kv"][i])
        m["sret"] = f(inputs["state_ret"][i])
        m["shg"] = f(inputs["state_hgrn"][i])
        m["cvec"] = np.ascontiguousarray(np.stack([f(inputs["c_ctx"]), f(inputs["c"][i])], axis=0))
        maps.append(m)
    return maps


def kernel(**inputs):
    if "nc" not in _CACHE:
        _CACHE["nc"] = build()
    nc = _CACHE["nc"]
    maps = make_in_maps(inputs)
    res = run_bass_kernel_spmd(nc, maps, core_ids=list(range(8)))
    r = res.results
    y = np.stack([x["yout"] for x in r], axis=0)
    y_prompt = y[:, 0].reshape(32, 256, D)
    y_sample = y[:, 1]
    nkv = np.concatenate([x["nkv"] for x in r], axis=0)
    nret = np.concatenate([x["nret"] for x in r], axis=0)
    nhg = np.concatenate([x["nhg"] for x in r], axis=0)
    return (np.ascontiguousarray(y_prompt, dtype=np.float32), np.ascontiguousarray(y_sample, dtype=np.float32),
            np.ascontiguousarray(nkv, dtype=np.float32), np.ascontiguousarray(nret, dtype=np.float32),
            np.ascontiguousarray(nhg, dtype=np.float32))
```

```python
import contextlib
import numpy as np
import concourse.bass as bass
import concourse.mybir as mybir
from concourse.bass_utils import run_bass_kernel_spmd

F32 = mybir.dt.float32
BF16 = mybir.dt.bfloat16
AF = mybir.ActivationFunctionType
ALU = mybir.AluOpType

ENGS = ("pe", "dve", "act", "pool", "sp")
D = 1024
DFF = 2816
EPS = 1e-6
XC = 896
W2W = 1920


class Op:
    __slots__ = ("eng", "fn", "deps", "inc", "semval", "dma_sem", "dma_val")

    def __init__(self, eng, fn):
        self.eng = eng
        self.fn = fn
        self.deps = set()
        self.inc = False
        self.semval = None
        self.dma_sem = None
        self.dma_val = None


class Prog:
    def __init__(self, nc):
        self.nc = nc
        self.ops = {e: [] for e in ENGS}
        self.last_w = {}
        self.readers = {}
        self.dma_sems = {}

    @staticmethod
    def _bank(k):
        if k == "psb":
            return 7
        if isinstance(k, tuple):
            if k[0] == "ps":
                return k[1]
            if k[0] == "pso":
                return 5
            if k[0] == "psd":
                return 6
        return None

    @staticmethod
    def _canon(keys):
        out = []
        for k in keys:
            if k == ("ps", 5):
                out += [("pso", 0), ("pso", 1)]
            elif k == ("ps", 6):
                out += [("psd", 0), ("psd", 1)]
            else:
                out.append(k)
        return out

    def op(self, eng, fn, reads=(), writes=(), dma_tag=None):
        o = Op(eng, fn)
        reads = self._canon(reads)
        writes = self._canon(writes)
        if eng != "pe":
            extra = [("rx", self._bank(k)) for k in reads if self._bank(k) is not None]
            if extra:
                writes = list(writes) + extra
        for k in reads:
            w = self.last_w.get(k)
            if w is not None:
                o.deps.add(w)
        for k in writes:
            w = self.last_w.get(k)
            if w is not None:
                o.deps.add(w)
            for r in self.readers.get(k, ()):
                o.deps.add(r)
        o.deps.discard(o)
        for k in reads:
            self.readers.setdefault(k, []).append(o)
        for k in writes:
            self.last_w[k] = o
            self.readers[k] = []
        if dma_tag is not None:
            cnt = self.dma_sems.setdefault(dma_tag, [0])
            cnt[0] += 16
            o.dma_sem = dma_tag
            o.dma_val = cnt[0]
        self.ops[eng].append(o)
        return o

    @staticmethod
    def _skip(o, d):
        return d.dma_sem is None and d.eng == o.eng and o.eng == "pe" and o.dma_sem is None

    def emit(self):
        nc = self.nc
        for e in ENGS:
            for o in self.ops[e]:
                for d in o.deps:
                    if d.dma_sem is not None or self._skip(o, d):
                        continue
                    d.inc = True
        for e in ENGS:
            c = 0
            for o in self.ops[e]:
                if o.dma_sem is None and o.inc:
                    c += 1
                    o.semval = c
        with contextlib.ExitStack() as st:
            esem = {e: st.enter_context(nc.semaphore("s_" + e)) for e in ENGS}
            dsem = {t: st.enter_context(nc.semaphore("d%d" % i)) for i, t in enumerate(self.dma_sems)}
            block = st.enter_context(nc.Block())

            def run_engine(e, eobj):
                waited = {}
                for o in self.ops[e]:
                    need = {}
                    for d in o.deps:
                        if d.dma_sem is not None:
                            key = ("d", d.dma_sem)
                            need[key] = max(need.get(key, 0), d.dma_val)
                        else:
                            if self._skip(o, d):
                                continue
                            key = ("e", d.eng)
                            need[key] = max(need.get(key, 0), d.semval)
                    for key, v in need.items():
                        if waited.get(key, 0) >= v:
                            continue
                        waited[key] = v
                        s = dsem[key[1]] if key[0] == "d" else esem[key[1]]
                        eobj.wait_ge(s, v)
                    ins = o.fn(eobj)
                    if o.dma_sem is not None:
                        ins.then_inc(dsem[o.dma_sem], 16)
                    elif o.inc:
                        ins.then_inc(esem[e], 1)
                if e == "sp":
                    for t, cnt in self.dma_sems.items():
                        eobj.wait_ge(dsem[t], cnt[0])
                    for e2 in ENGS:
                        if e2 == "sp":
                            continue
                        last = None
                        for o2 in self.ops[e2]:
                            if o2.semval is not None:
                                last = o2.semval
                        if last:
                            eobj.wait_ge(esem[e2], last)

            block.tensor(lambda eng: run_engine("pe", eng))
            block.vector(lambda eng: run_engine("dve", eng))
            block.scalar(lambda eng: run_engine("act", eng))
            block.gpsimd(lambda eng: run_engine("pool", eng))
            block.sync(lambda eng: run_engine("sp", eng))


def _host_consts():
    c = {}
    c["ident"] = np.eye(128, dtype=np.float32)
    c["onesmean"] = np.full((128, 128), 1.0 / 1024, np.float32)
    c["om128"] = np.full((128, 128), 1.0 / 128, np.float32)
    bd = np.zeros((128, 128), np.float32)
    bd[:64, :64] = 1.0 / 64
    bd[64:, 64:] = 1.0 / 64
    c["bd64"] = bd
    c["ones"] = np.ones((128, 128), np.float32)
    oz = np.zeros((2, 128, 128), np.float32)
    oz[0, :, :64] = 1.0
    oz[1, :, 64:] = 1.0
    c["onesz0"] = oz[0]
    c["onesz1"] = oz[1]
    rt = np.zeros((128, 128), np.float32)
    for i in range(128):
        if i % 32 < 16:
            rt[i + 16, i] = -1.0
        else:
            rt[i - 16, i] = 1.0
    c["rotT"] = rt
    t = np.arange(1024)
    inv = (10000.0 ** (-np.arange(16, dtype=np.float32) / 16)).astype(np.float32)
    cs = np.zeros((128, 1024), np.float32)
    sn = np.zeros((128, 1024), np.float32)
    for i in range(128):
        d = i % 64
        pos = (t // 64) if d < 32 else (t % 64)
        ang = pos.astype(np.float32) * inv[d % 16]
        cs[i] = np.cos(ang)
        sn[i] = np.sin(ang)
    c["cos"] = cs
    c["sin"] = sn
    ki = np.arange(128)[:, None]
    x = np.arange(W2W)[None, :]
    delta = (x - ki - XC).astype(np.float32)
    c["dpp"] = np.where(delta >= 0, delta, 1e7).astype(np.float32)
    c["dnn"] = np.where(delta <= 0, -delta, 1e7).astype(np.float32)
    c["ef"] = np.broadcast_to((np.arange(1024, dtype=np.float32) + 1.0)[None, :], (128, 1024)).copy()
    m = (np.arange(2)[None, :] * 128 + np.arange(128)[:, None]).astype(np.float32)
    c["epd"] = np.stack([255.0 - m, m], axis=1).astype(np.float32)
    p = np.arange(128)[:, None] % 32
    tt = np.arange(32)[None, :]
    c["hmask"] = np.stack([(tt >= p), (tt <= p)], axis=1).astype(np.float32)
    return c


def _nabias(rpb):
    out = np.empty((2, 8, 2, 6, 128, 512), np.float32)
    flat = np.concatenate([rpb.reshape(2, 8, -1), np.full((2, 8, 1), -1e30, np.float32)], axis=-1)
    PAD = 15 * 31
    for hf in range(2):
        for kt in range(6):
            pr = 2 * hf + kt
            kp = np.arange(128)
            kr = (2 * pr + kp // 64)[:, None]
            kc = (kp % 64)[:, None]
            ql = np.arange(512)
            r = (8 * hf + ql // 64)[None, :]
            cc = (ql % 64)[None, :]
            rs = np.clip(r - 4, 0, 8)
            rowv = (kr >= rs) & (kr < rs + 8)
            win0 = np.clip(cc - 8, 0, 48)
            colv = (kc >= win0) & (kc < win0 + 16)
            ro = kr - r + 7
            co = np.clip(kc - cc, -15, 15) + 15
            idx = np.where(rowv & colv, np.clip(ro, 0, 14) * 31 + co, PAD)
            out[:, :, hf, kt] = flat[:, :, idx]
    return out


def build(depth=4, groups=(0, 1), parts=("mod", "pre", "mixer", "ffn")):
    nc = bass.Bass("TRN2", target_bir_lowering=False)
    dt = lambda n, s, kind="ExternalInput": nc.dram_tensor(n, list(s), F32, kind=kind).ap()
    xin = dt("xin", [2, 1024, D])
    ckv = dt("ckv", [2, 2, 8, 256, 64])
    sret = dt("sret", [2, 2, 8, 64, 64])
    shg = dt("shg", [2, 2, 8, 128, 128])
    cvec = dt("cvec", [2, D])
    w_mod = dt("w_mod", [4, D, 6 * D])
    b_mod = dt("b_mod", [4, 6 * D])
    norm_g = dt("norm_g", [4, 4, D])
    w_in_even = dt("w_in_even", [2, D, 3584])
    w_out_even = dt("w_out_even", [2, D, D])
    ret_decay = dt("ret_decay", [32])
    w_in_odd = dt("w_in_odd", [2, D, 5120])
    w_out_odd = dt("w_out_odd", [2, D, D])
    hgrn_lb = dt("hgrn_lb", [2, 2, D])
    hgrn_gnorm = dt("hgrn_gnorm", [2, 128])
    w_ffn_in = dt("w_ffn_in", [4, D, 2 * DFF])
    w_ffn_out = dt("w_ffn_out", [4, DFF, D])
    nab = dt("nab", [2, 8, 2, 6, 128, 512])
    hc = {k: dt("c_" + k, v.shape) for k, v in _host_consts().items()}
    yout = dt("yout", [2, 1024, D], "ExternalOutput")
    nkv = dt("nkv", [4, 2, 2, 8, 256, 64], "ExternalOutput")
    nret = dt("nret", [4, 2, 2, 8, 64, 64], "ExternalOutput")
    nhg = dt("nhg", [4, 2, 2, 8, 128, 128], "ExternalOutput")

    st = contextlib.ExitStack()
    _n = [0]

    def sb(shape, dtype=F32, name=None):
        _n[0] += 1
        return st.enter_context(nc.sbuf_tensor("t%d_%s" % (_n[0], name or ""), list(shape), dtype))

    P = Prog(nc)
    st.enter_context(nc.allow_non_contiguous_dma(reason="small transposed parameter loads"))

    XT = sb([128, 8, 1024], F32, "XT")
    Hh = sb([128, 8192], BF16, "H")
    H_bf = Hh[:].rearrange("p (c t) -> p c t", t=1024)
    H_f = Hh[:].bitcast(F32).rearrange("p (m t) -> p m t", t=512)
    Uu = sb([128, 22 * 1024], BF16, "U")
    U_bf = Uu[:].rearrange("p (j t) -> p j t", t=1024)
    U_f = Uu[:].bitcast(F32)
    wbuf = [sb([128, 4096], BF16, "wbuf%d" % i) for i in range(3)]
    sq = sb([128, 8, 512], BF16, "sq")
    pT2 = sb([128, 4096], BF16, "pT")
    pTt = pT2[:].rearrange("p (i t) -> p i t", t=512)
    pT_f = pT2[:].bitcast(F32).rearrange("p (s t) -> p s t", t=1024)
    rt_t = sb([128, 512], F32, "rt")
    rstd_t = sb([128, 512], F32, "rstd")
    tmp_t = [sb([128, 512], F32, "tmp%d" % i) for i in range(2)]
    eps_t = sb([128, 1], F32, "eps")
    epoch_t = sb([128, 1], F32, "epoch")
    nln8 = sb([128, 1], F32, "nln8")
    ident_f = sb([128, 128], F32, "identf")
    ident_b = sb([128, 128], BF16, "identb")
    onesmean = sb([128, 128], BF16, "onesmean")
    om128 = sb([128, 128], BF16, "om128")
    bd64 = sb([128, 128], BF16, "bd64")
    ones_b = sb([128, 128], BF16, "ones")
    onesz = [sb([128, 128], BF16, "onesz%d" % i) for i in range(2)]
    rotT = sb([128, 128], BF16, "rotT")
    hmask = sb([128, 2, 32], BF16, "hmask")
    epd = sb([128, 2, 2], F32, "epd")
    cT = sb([128, 2, 8], F32, "cT")
    scT = sb([128, 2, 8], BF16, "scT")
    bmT = sb([128, 4, 48], F32, "bmT")
    ngT = sb([128, 4, 4, 8], F32, "ngT")
    modv = sb([128, 4, 48, 2], F32, "modv")
    modd = sb([128, 4, 6, 8, 2], F32, "modd")
    mtmp = sb([128, 8, 2], F32, "mtmp")
    rd = sb([128, 32], F32, "rd")
    lg = sb([128, 32], F32, "lg")
    nlg = sb([128, 32], F32, "nlg")
    lg1025 = sb([128, 32], F32, "lg1025")
    lbT = sb([128, 2, 2, 8], F32, "lbT")
    lowt = sb([128, 2, 2, 8], F32, "lowt")
    omlt = sb([128, 2, 2, 8], F32, "omlt")
    nomlt = sb([128, 2, 2, 8], F32, "nomlt")
    gnT = sb([128, 2], F32, "gnT")
    ARENA = 59 * 1024
    arena = sb([128, ARENA // 2], BF16, "arena")

    ps = [st.enter_context(nc.psum_tensor("ps%d" % i, [128, 512], F32)) for i in range(7)]
    psb = st.enter_context(nc.psum_tensor("psb", [128, 1024], BF16))
    _rr = [0]

    def nextps():
        i = _rr[0] % 5
        _rr[0] += 1
        return ps[i], ("ps", i)

    PSO, PSD = ps[5], ps[6]

    class Carver:
        def __init__(self):
            self.off = 0

        def take(self, nbytes, dtype, shape_tail=None):
            assert self.off % 4 == 0
            n_el = nbytes // (2 if dtype == BF16 else 4)
            a = arena[:, self.off // 2: self.off // 2 + nbytes // 2]
            if dtype == F32:
                a = a.bitcast(F32)
            self.off += nbytes
            assert self.off <= ARENA, self.off
            return a

    def dma_sp(out, in_, reads, writes, tag):
        P.op("sp", lambda e: e.dma_start(out=out, in_=in_), reads=reads, writes=writes, dma_tag=tag)

    def dma_pool(out, in_, reads, writes, tag):
        P.op("pool", lambda e: e.dma_start(out=out, in_=in_), reads=reads, writes=writes, dma_tag=tag)

    def act(out, in_, func, reads, writes, scale=None, bias=None):
        kw = {}
        if scale is not None:
            kw["scale"] = scale
        if bias is not None:
            kw["bias"] = bias
        P.op("act", lambda e: e.activation(out=out, in_=in_, func=func, **kw), reads=reads, writes=writes)

    def mm(out, lhsT, rhs, start, stop, reads, writes, tp=None):
        if tp is None:
            P.op("pe", lambda e: e.matmul(out, lhsT=lhsT, rhs=rhs, start=start, stop=stop), reads=reads, writes=writes)
        else:
            P.op("pe", lambda e: e.matmul(out, lhsT=lhsT, rhs=rhs, start=start, stop=stop, tile_position=tp), reads=reads, writes=writes)

    def dve(fn, reads, writes):
        P.op("dve", fn, reads=reads, writes=writes)

    _wi = [0]

    def load_w(pieces, kc):
        i = _wi[0] % 3
        _wi[0] += 1
        ntot = sum(p.shape[1] for p in pieces)
        assert kc * ntot <= 4096
        view = wbuf[i][:, 0:kc * ntot].rearrange("p (k n) -> p k n", n=ntot)
        off = 0
        for p in pieces:
            n = p.shape[1]
            src = p.rearrange("(k p) n -> p k n", p=128)
            dma_pool(view[:, :, off:off + n], src, [], [("wb", i)], ("wb", i))
            off += n
        return view, ("wb", i)

    cl = 0

    def cload(dst, src, key, cast=False):
        nonlocal cl
        cl += 1
        if cast:
            dma_pool(dst, src, [], [key], ("c", cl))
        else:
            dma_sp(dst, src, [], [key], ("c", cl))

    cload(ident_f[:], hc["ident"], "identf")
    cload(ident_b[:], hc["ident"], "identb", True)
    cload(onesmean[:], hc["onesmean"], "onesmean", True)
    cload(om128[:], hc["om128"], "om128", True)
    cload(bd64[:], hc["bd64"], "bd64", True)
    cload(ones_b[:], hc["ones"], "ones", True)
    dma_pool(onesz[0][:], hc["onesz0"], [], ["onesz"], ("c", "onesz"))
    dma_pool(onesz[1][:], hc["onesz1"], [], ["onesz"], ("c", "onesz"))
    cload(rotT[:], hc["rotT"], "rotT", True)
    cload(hmask[:], hc["hmask"], "hmask", True)
    cload(epd[:], hc["epd"], "epd")
    for j in range(2):
        dma_sp(cT[:, j, :], cvec[j].rearrange("(k p) -> p k", p=128), [], ["cT"], ("c", "cT"))
    for l in range(4):
        dma_sp(bmT[:, l, :], b_mod[l].rearrange("(m p) -> p m", p=128), [], ["bmT"], ("c", "bmT"))
        for i in range(4):
            dma_sp(ngT[:, l, i, :], norm_g[l, i].rearrange("(c p) -> p c", p=128), [], ["ngT"], ("c", "ngT"))
    for o in range(2):
        for d_ in range(2):
            dma_sp(lbT[:, o, d_, :], hgrn_lb[o, d_].rearrange("(h p) -> p h", p=128), [], ["lbT"], ("c", "lbT"))
    cload(rd[:], ret_decay.partition_broadcast(128), "rd")
    cload(gnT[:], hgrn_gnorm.rearrange("o p -> p o"), "gnT")
    dve(lambda e: e.memset(eps_t[:], EPS), [], ["eps"])
    dve(lambda e: e.memset(nln8[:], -2.0794415416798357), ["eps"], ["eps"])
    act(lg[:], rd[:], AF.Exp, ["rd"], ["lg"])
    dve(lambda e: e.tensor_scalar_mul(nlg[:], lg[:], 1.0), ["lg"], ["nlg"])
    dve(lambda e: e.tensor_scalar_mul(lg[:], nlg[:], -1.0), ["nlg", "lg"], ["lg"])
    dve(lambda e: e.tensor_scalar_mul(lg1025[:], lg[:], 1025.0), ["lg"], ["lg1025"])
    dve(lambda e: e.memset(lowt[:], 0.0), [], ["lowt"])
    dve(lambda e: e.tensor_sub(lowt[:, 1], lbT[:, 1], lbT[:, 0]), ["lbT", "lowt"], ["lowt"])
    act(lowt[:, 1], lowt[:, 1], AF.Sigmoid, ["lowt"], ["lowt"])
    dve(lambda e: e.memset(lowt[:, 0], 0.0), ["lowt"], ["lowt"])
    dve(lambda e: e.tensor_scalar(omlt[:], lowt[:], -1.0, 1.0, op0=ALU.mult, op1=ALU.add), ["lowt"], ["omlt"])
    dve(lambda e: e.tensor_scalar_mul(nomlt[:], omlt[:], -1.0), ["omlt"], ["nomlt"])

    act(scT[:], cT[:], AF.Silu, ["cT"], ["scT"])

    def mod_tile(l, mq):
        pm, pmk = PSD, "psd_mod"
        wt, wk = load_w([w_mod[l][:, mq * 512:(mq + 1) * 512]], 8)
        for mi in range(4):
            mc = mq * 4 + mi
            for k in range(8):
                mm(pm[:, mc * 2:mc * 2 + 2], wt[:, k, mi * 128:(mi + 1) * 128], scT[:, :, k], k == 0, k == 7,
                   [wk, "scT"], [("psd", 0), ("psd", 1)])

    def mod_finish(l):
        pm = PSD
        pmk2 = [("psd", 0), ("psd", 1)]
        dve(lambda e, l=l, pm=pm: e.tensor_tensor(out=modv[:, l], in0=pm[:, 0:96].rearrange("p (m j) -> p m j", j=2),
                                                  in1=bmT[:, l, :].unsqueeze(2).to_broadcast([128, 48, 2]), op=ALU.add),
            pmk2 + ["bmT"], [("modv", l)])
        for idx, (kind, mo, gi) in enumerate([("a", 8, 0), ("b", 0, None), ("g", 16, 1), ("a", 32, 2), ("b", 24, None), ("g", 40, 3)]):
            src = modv[:, l, mo:mo + 8, :]
            dst = modd[:, l, idx]
            if kind == "b":
                dve(lambda e, dst=dst, src=src: e.tensor_copy(out=dst, in_=src), [("modv", l)], [("modd", l, idx)])
            else:
                gb = ngT[:, l, gi, :].unsqueeze(2).to_broadcast([128, 8, 2])
                if kind == "a":
                    dve(lambda e, src=src: e.tensor_scalar_add(mtmp[:], src, 1.0), [("modv", l)], ["mtmp"])
                    dve(lambda e, dst=dst, gb=gb: e.tensor_tensor(out=dst, in0=mtmp[:], in1=gb, op=ALU.mult), ["mtmp", "ngT"], [("modd", l, idx)])
                else:
                    dve(lambda e, dst=dst, gb=gb, src=src: e.tensor_tensor(out=dst, in0=src, in1=gb, op=ALU.mult), [("modv", l), "ngT"], [("modd", l, idx)])

    def XTk(c, tb):
        return ("XT", c, tb)

    def XTa(c, tb):
        return XT[:, c, tb * 512:(tb + 1) * 512]

    def Hk(c, tb):
        return ("H", 2 * c + tb)

    def Ha(c, tb):
        return H_bf[:, c, tb * 512:(tb + 1) * 512]

    def Uk(j, tb):
        return ("U", 2 * j + tb)

    def Ua(j, tb):
        return U_bf[:, j, tb * 512:(tb + 1) * 512]

    def rms_stats(ones_mat, ones_key, nchunks, sqkeys):
        pa, pak = nextps()
        for c in range(nchunks):
            mm(pa[:], ones_mat[:], sq[:, c, :], c == 0, c == nchunks - 1, [ones_key, sqkeys[c]], [pak])
        act(rt_t[:], pa[:], AF.Ln, [pak, "eps"], ["rt"], bias=eps_t[:, 0:1])
        act(rstd_t[:], rt_t[:], AF.Exp, ["rt"], ["rstd"], scale=-0.5)

    def prenorm(l, which, g):
        ai, bi = (0, 1) if which == 0 else (3, 4)
        for tb in range(2):
            for c in range(8):
                act(sq[:, c, :], XTa(c, tb), AF.Square, [XTk(c, tb)], [("sq", c)])
            rms_stats(onesmean, "onesmean", 8, [("sq", c) for c in range(8)])
            for c in range(8):
                t = tmp_t[c % 2]
                tk = ("tmp", c % 2)
                dve(lambda e, t=t, c=c, tb=tb: e.tensor_tensor(out=t[:], in0=XTa(c, tb), in1=rstd_t[:], op=ALU.mult),
                    [XTk(c, tb), "rstd"], [tk])
                act(Ha(c, tb), t[:], AF.Identity, [tk, ("modd", l, ai), ("modd", l, bi)], [Hk(c, tb)],
                    scale=modd[:, l, ai, c, g:g + 1], bias=modd[:, l, bi, c, g:g + 1])

    def outproj_postnorm(l, g, wap, kc, srcfn, ggi):
        for tb in range(2):
            for m in range(8):
                wt, wk = load_w([wap[:, m * 128:(m + 1) * 128]], kc)
                pm, pmk = nextps()
                for k in range(kc):
                    sa, sk = srcfn(k, tb)
                    mm(pm[:], wt[:, k, :], sa, k == 0, k == kc - 1, [wk, sk], [pmk])
                act(H_f[:, m, :], pm[:], AF.Copy, [pmk], [("H", 2 * m), ("H", 2 * m + 1)])
                act(sq[:, m, :], pm[:], AF.Square, [pmk], [("sq", m)])
            rms_stats(onesmean, "onesmean", 8, [("sq", c) for c in range(8)])
            for m in range(8):
                t = tmp_t[m % 2]
                tk = ("tmp", m % 2)
                dve(lambda e, t=t, m=m: e.tensor_tensor(out=t[:], in0=H_f[:, m, :], in1=rstd_t[:], op=ALU.mult),
                    [("H", 2 * m), ("H", 2 * m + 1), "rstd"], [tk])
                dve(lambda e, t=t, m=m, tb=tb: e.scalar_tensor_tensor(out=XTa(m, tb), in0=t[:], scalar=modd[:, l, ggi, m, g:g + 1],
                                                                       in1=XTa(m, tb), op0=ALU.mult, op1=ALU.add),
                    [tk, XTk(m, tb), ("modd", l, ggi)], [XTk(m, tb)])

    def ffn(l, g):
        wi = w_ffn_in[l]
        do_mod = ("mod" in parts) and g == groups[0] and l + 1 < depth
        for j in range(22):
            if do_mod and 4 <= j < 16:
                mod_tile(l + 1, j - 4)
                if j - 4 == 11:
                    mod_finish(l + 1)
            wt, wk = load_w([wi[:, j * 128:(j + 1) * 128], wi[:, DFF + j * 128:DFF + (j + 1) * 128]], 8)
            for tb in range(2):
                pa, pak = nextps()
                pu, puk = nextps()
                for k in range(8):
                    mm(pa[:], wt[:, k, 0:128], Ha(k, tb), k == 0, k == 7, [wk, Hk(k, tb)], [pak])
                for k in range(8):
                    mm(pu[:], wt[:, k, 128:256], Ha(k, tb), k == 0, k == 7, [wk, Hk(k, tb)], [puk])
                t = tmp_t[tb]
                tk = ("tmp", tb)
                act(t[:], pa[:], AF.Silu, [pak], [tk])
                dve(lambda e, t=t, pu=pu, j=j, tb=tb: e.tensor_tensor(out=Ua(j, tb), in0=t[:], in1=pu[:], op=ALU.mult),
                    [tk, puk], [Uk(j, tb)])
        outproj_postnorm(l, g, w_ffn_out[l], 22, lambda k, tb: (Ua(k, tb), Uk(k, tb)), 5)

    def load_group(g):
        for tt in range(8):
            s = tt % 2
            dma_sp(pT_f[:, s, :], xin[g, tt * 128:(tt + 1) * 128, :], [], [("pT", 4 * s + i) for i in range(4)], ("xst", s))
            for hb in range(2):
                pb, pbk = nextps()
                for ci in range(4):
                    c = hb * 4 + ci
                    P.op("pe", lambda e, pb=pb, ci=ci, s=s, c=c: e.transpose(out=pb[:, ci * 128:(ci + 1) * 128], in_=pT_f[:, s, c * 128:(c + 1) * 128], identity=ident_f[:]),
                         reads=[("pT", 4 * s + i) for i in range(4)] + ["identf"], writes=[pbk])
                tb = tt // 4
                dstv = XT[:, hb * 4:hb * 4 + 4, tt * 128:(tt + 1) * 128]
                act(dstv, pb[:].rearrange("p (c t) -> p c t", t=128), AF.Copy, [pbk], [XTk(hb * 4 + ci, tb) for ci in range(4)])

    def store_group(g):
        for tt in range(8):
            s = tt % 2
            tb = tt // 4
            for hb in range(2):
                pb, pbk = nextps()
                for ci in range(4):
                    c = hb * 4 + ci
                    P.op("pe", lambda e, pb=pb, ci=ci, c=c, tt=tt: e.transpose(out=pb[:, ci * 128:(ci + 1) * 128], in_=XT[:, c, tt * 128:(tt + 1) * 128], identity=ident_f[:]),
                         reads=[XTk(c, tb), "identf"], writes=[pbk])
                act(pT_f[:, s, hb * 512:(hb + 1) * 512], pb[:], AF.Copy, [pbk], [("pT", 4 * s + 2 * hb), ("pT", 4 * s + 2 * hb + 1)])
            dma_sp(yout[g, tt * 128:(tt + 1) * 128, :], pT_f[:, s, :], [("pT", 4 * s + i) for i in range(4)], [], ("yst", s))

    def run_pipeline(tiles, stage1, stage2, look=3, extras=None):
        n = len(tiles)
        for i in range(min(look, n)):
            stage1(tiles[i])
        for i in range(n):
            if i + look < n:
                stage1(tiles[i + look])
            stage2(tiles[i])
            if extras and i in extras:
                extras[i]()

    def even_mixer(l, g):
        e_i = l // 2
        wie = w_in_even[e_i]
        cv = Carver()
        qT = cv.take(2048, BF16)
        kTz = [cv.take(2048, BF16) for _ in range(2)]
        gsT = cv.take(2048, BF16)
        vtz = [cv.take(2048, BF16).rearrange("p (t d) -> p t d", d=128) for _ in range(2)]
        w2 = cv.take(2 * W2W * 2, BF16).rearrange("p (h x) -> p h x", x=W2W)
        w2tmp = cv.take(W2W * 2, BF16)
        osq = cv.take(1024, BF16)
        if g == 0:
            ktm = cv.take(2048, BF16).rearrange("p (t d) -> p t d", d=128)
            kdec = cv.take(2048, BF16).rearrange("p (t d) -> p t d", d=128)
            kvst1 = cv.take(4096, F32).rearrange("p (t d) -> p t d", d=256)
            retst = [cv.take(2048, F32) for _ in range(2)]
            decP = cv.take(128, F32).rearrange("p (d t h) -> p d t h", d=2, t=2)
        else:
            qraw = cv.take(2048, BF16)
            kraw = cv.take(2048, BF16)
            bias_t = [cv.take(2048, F32) for _ in range(2)]
            stmp = [cv.take(2048, F32) for _ in range(2)]
            decS = cv.take(4096, BF16).rearrange("p (d n) -> p d n", n=1024)
            pst = [cv.take(1024, BF16) for _ in range(2)]
            s0z = cv.take(2048, BF16).rearrange("p (d c v) -> p d c v", d=2, c=4)
            ctxK = cv.take(2048, BF16).rearrange("p (t d) -> p t d", d=512)
            ctxV = cv.take(2048, BF16).rearrange("p (t d) -> p t d", d=512)
            ctxKTz = [cv.take(2048, BF16).rearrange("p (c k) -> p c k", k=256) for _ in range(2)]
            ctxVz = [cv.take(2048, BF16).rearrange("p (t c d) -> p t c d", t=2, c=4) for _ in range(2)]
        cos_t = U_f[:, 4096:5120]
        sin_t = U_f[:, 5120:6144]
        ef = U_f[:, 6144:7168]
        UK_C = [("U", s) for s in range(16, 28)]
        dpp = U_f[:, 7168:7168 + W2W]
        dnn = U_f[:, 9088:9088 + W2W]
        UK_T = [("U", s) for s in range(28, 44)]
        dve(lambda e: e.memset(epoch_t[:], 0.0), [], ["arena_epoch"])
        dma_sp(dpp, hc["dpp"], [], UK_T, ("c", "dpp"))
        dma_sp(dnn, hc["dnn"], [], UK_T, ("c", "dnn"))
        dve(lambda e: e.memset(kTz[0][64:128, :], 0.0), [], [("kT", 0), ("kT", 1)])
        dve(lambda e: e.memset(kTz[1][0:64, :], 0.0), [("kT", 0), ("kT", 1)], [("kT", 0), ("kT", 1)])
        vk_all = [("vtm", tt) for tt in range(8)]
        dve(lambda e: e.memset(vtz[0][:, :, 64:128], 0.0), [], vk_all)
        dve(lambda e: e.memset(vtz[1][:, :, 0:64], 0.0), vk_all, vk_all)
        if g == 1:
            dma_sp(cos_t, hc["cos"], [], UK_C, ("c", "cos"))
            dma_sp(sin_t, hc["sin"], [], UK_C, ("c", "sin"))
            dma_sp(ef, hc["ef"], [], UK_C, ("c", "ef"))
            for t_ in range(2):
                dma_pool(ctxK[:, t_, :].rearrange("p (h d) -> p h d", d=64), ckv[e_i, 0][:, t_ * 128:(t_ + 1) * 128, :].rearrange("h p d -> p h d"), ["arena_epoch"], ["ctxK"], ("c", "ctxK"))
                dma_pool(ctxV[:, t_, :].rearrange("p (h d) -> p h d", d=64), ckv[e_i, 1][:, t_ * 128:(t_ + 1) * 128, :].rearrange("h p d -> p h d"), ["arena_epoch"], ["ctxV"], ("c", "ctxV"))
            for cp in range(4):
                for kt in range(2):
                    P.op("pe", lambda e, cp=cp, kt=kt: e.transpose(out=psb[:, (cp * 2 + kt) * 128:(cp * 2 + kt + 1) * 128], in_=ctxK[:, kt, cp * 128:(cp + 1) * 128], identity=ident_b[:]),
                         reads=["ctxK", "identb"], writes=["psb"])
            psbv = psb[:].rearrange("p (c k) -> p c k", k=256)
            dve(lambda e: e.memset(ctxKTz[0][64:128], 0.0), [], ["ctxKT"])
            dve(lambda e: e.memset(ctxKTz[1][0:64], 0.0), ["ctxKT"], ["ctxKT"])
            act(ctxKTz[0][0:64], psbv[0:64], AF.Copy, ["psb", "ctxKT"], ["ctxKT"])
            act(ctxKTz[1][64:128], psbv[64:128], AF.Copy, ["psb", "ctxKT"], ["ctxKT"])
            cvv = ctxV.rearrange("p t (c d) -> p t c d", d=128)
            dve(lambda e: e.memset(ctxVz[0][:, :, :, 64:128], 0.0), [], ["ctxVz"])
            dve(lambda e: e.memset(ctxVz[1][:, :, :, 0:64], 0.0), ["ctxVz"], ["ctxVz"])
            dve(lambda e: e.tensor_copy(out=ctxVz[0][:, :, :, 0:64], in_=cvv[:, :, :, 0:64]), ["ctxV", "ctxVz"], ["ctxVz"])
            dve(lambda e: e.tensor_copy(out=ctxVz[1][:, :, :, 64:128], in_=cvv[:, :, :, 64:128]), ["ctxV", "ctxVz"], ["ctxVz"])
            dve(lambda e: e.memset(s0z, 0.0), [], ["s0"])
            for dr in range(2):
                for hh in range(2):
                    dma_pool(s0z[64 * hh:64 * hh + 64, dr, :, 64 * hh:64 * hh + 64], sret[e_i, dr].rearrange("(c two) k v -> two k c v", two=2)[hh], ["s0", "arena_epoch"], ["s0"], ("c", "s0"))
        else:
            for dr in range(2):
                for h in range(8):
                    col = e_i * 16 + dr * 8 + h
                    dve(lambda e, dr=dr, h=h, col=col: e.tensor_scalar(decP[:, dr, :, h], epd[:, dr, :], lg[:, col:col + 1], None, op0=ALU.mult), ["epd", "lg"], ["decP"])
            act(decP, decP, AF.Exp, ["decP", "eps"], ["decP"], bias=nln8[:, 0:1])
        PSOK = [("pso", 0), ("pso", 1)]
        PSDK = [("psd", 0), ("psd", 1)]

        def proj_fm(wt, wk, col, dst, dstk, func=AF.Copy):
            for tb in range(2):
                pp, ppk = nextps()
                for k in range(8):
                    mm(pp[:], wt[:, k, col:col + 128], Ha(k, tb), k == 0, k == 7, [wk, Hk(k, tb)], [ppk])
                act(dst[:, tb * 512:(tb + 1) * 512], pp[:], func, [ppk], [(dstk, tb)])

        def proj_k(wt, wk, col):
            for tb in range(2):
                sl = slice(tb * 512, (tb + 1) * 512)
                pp, ppk = nextps()
                for k in range(8):
                    mm(pp[:], wt[:, k, col:col + 128], Ha(k, tb), k == 0, k == 7, [wk, Hk(k, tb)], [ppk])
                act(kTz[0][0:64, sl], pp[0:64, :], AF.Copy, [ppk], [("kT", tb)])
                act(kTz[1][64:128, sl], pp[64:128, :], AF.Copy, [ppk, ("kT", tb)], [("kT", tb)])

        def rope(raw, rawk, dsts, dstk):
            for tb in range(2):
                pp, ppk = nextps()
                sl = slice(tb * 512, (tb + 1) * 512)
                mm(pp[:], rotT[:], raw[:, sl], True, True, ["rotT", (rawk, tb)], [ppk])
                t0, t1 = tmp_t
                dve(lambda e, sl=sl: e.tensor_tensor(out=t0[:], in0=raw[:, sl], in1=cos_t[:, sl], op=ALU.mult), [(rawk, tb)] + UK_C, [("tmp", 0)])
                dve(lambda e, sl=sl, pp=pp: e.tensor_tensor(out=t1[:], in0=pp[:], in1=sin_t[:, sl], op=ALU.mult), [ppk] + UK_C, [("tmp", 1)])
                if len(dsts) == 1:
                    dve(lambda e, sl=sl: e.tensor_tensor(out=dsts[0][:, sl], in0=t0[:], in1=t1[:], op=ALU.add), [("tmp", 0), ("tmp", 1)], [(dstk, tb)])
                else:
                    for hh in range(2):
                        pr = slice(64 * hh, 64 * hh + 64)
                        dve(lambda e, sl=sl, pr=pr, hh=hh: e.tensor_tensor(out=dsts[hh][pr, sl], in0=t0[pr, :], in1=t1[pr, :], op=ALU.add), [("tmp", 0), ("tmp", 1), (dstk, tb)], [(dstk, tb)])

        def proj_tm(wt, wk, want_k):
            for tt in range(8):
                tb = tt // 4
                pp, ppk = nextps()
                for k in range(8):
                    mm(pp[:, 0:256], H_bf[:, k, tt * 128:(tt + 1) * 128], wt[:, k, 128:384], k == 0, k == 7, [wk, Hk(k, tb)], [ppk])
                act(vtz[0][:, tt, 0:64], pp[:, 128:192], AF.Copy, [ppk], [("vtm", tt)])
                act(vtz[1][:, tt, 64:128], pp[:, 192:256], AF.Copy, [ppk, ("vtm", tt)], [("vtm", tt)])
                yield tt, pp, ppk

        _bi = [0]
        _pi = [0]

        def npi():
            pi = _pi[0] % 8
            _pi[0] += 1
            return pi

        def mk_tiles():
            tl = []
            for tb in range(2):
                if g == 0:
                    for bl in range(2):
                        for hh in range(2):
                            tl.append(dict(tb=tb, hh=hh, bl=bl, last=(bl == 1 and hh == 1)))
                else:
                    for hh in range(2):
                        for kti in range(8):
                            tl.append(dict(tb=tb, hh=hh, kti=kti, last=(hh == 1 and kti == 7)))
            return tl

        for cp in range(4):
            wt, wk = load_w([wie[:, cp * 128:(cp + 1) * 128], wie[:, 512 + cp * 128:512 + (cp + 1) * 128],
                             wie[:, 1024 + cp * 128:1024 + (cp + 1) * 128]], 8)
            proj_fm(wt, wk, 0, qT, "qT")
            proj_k(wt, wk, 128)
            for tt, pp, ppk in proj_tm(wt, wk, True):
                if g == 0:
                    dve(lambda e, pp=pp, tt=tt: e.tensor_copy(out=kvst1[:, tt % 4, :], in_=pp[:, 0:256]), [ppk], [("kvst", 0)])
                    if tt % 4 == 3 and "noKV" not in parts:
                        sI = tt // 4
                        for bl in range(2):
                            b = 2 * sI + bl
                            for kvi in range(2):
                                for hh_ in range(2):
                                    src = kvst1[:, 2 * bl:2 * bl + 2, kvi * 128 + hh_ * 64:kvi * 128 + hh_ * 64 + 64]
                                    dst = nkv[b, e_i, kvi, 2 * cp + hh_].rearrange("(t p) d -> p t d", p=128)
                                    dma_sp(dst, src, [("kvst", 0)], [], ("kvst", 0))
            tilesA = mk_tiles()

            def a_stage1(t):
                tb, hh = t["tb"], t["hh"]
                h = 2 * cp + hh
                pS, pSk = nextps()
                pi = npi()
                t["pi"] = pi
                if g == 0:
                    b = 2 * tb + t["bl"]
                    qs = slice(b * 256, (b + 1) * 256)
                    for kt in range(2):
                        ks = slice((2 * b + kt) * 128, (2 * b + kt + 1) * 128)
                        mm(pS[:, kt * 256:(kt + 1) * 256], kTz[hh][:, ks], qT[:, qs], True, True, [("kT", tb), ("qT", tb)], [pSk])
                    act(pTt[:, pi, :], pS[:], AF.Exp, [pSk], [("pT", pi)], scale=0.125)
                else:
                    hf = tb
                    kti = t["kti"]
                    qs = slice(hf * 512, (hf + 1) * 512)
                    if kti < 6:
                        tt = 2 * hf + kti
                        ks = slice(tt * 128, (tt + 1) * 128)
                        mm(pS[:], kTz[hh][:, ks], qT[:, qs], True, True, [("kT", tt // 4), ("qT", tb)], [pSk])
                        bi = _bi[0] % 2
                        _bi[0] += 1
                        dma_sp(bias_t[bi], nab[e_i, h, hf, kti], ["arena_epoch"], [("bias", bi)], ("bias", bi))
                        stt = stmp[bi]
                        dve(lambda e, stt=stt, pS=pS, bi=bi: e.scalar_tensor_tensor(out=stt, in0=pS[:], scalar=0.125, in1=bias_t[bi], op0=ALU.mult, op1=ALU.add),
                            [pSk, ("bias", bi)], [("stmp", bi)])
                        act(pTt[:, pi, :], stt, AF.Exp, [("stmp", bi)], [("pT", pi)])
                    else:
                        kt = kti - 6
                        mm(pS[:], ctxKTz[hh][:, cp, kt * 128:(kt + 1) * 128], qT[:, qs], True, True, ["ctxKT", ("qT", tb)], [pSk])
                        act(pTt[:, pi, :], pS[:], AF.Exp, [pSk], [("pT", pi)], scale=0.125)

            def a_stage2(t):
                tb, hh, pi = t["tb"], t["hh"], t["pi"]
                if g == 0:
                    bl = t["bl"]
                    b = 2 * tb + bl
                    for kt in range(2):
                        tt = 2 * b + kt
                        st_ = (hh == 0 and kt == 0)
                        sp_ = (hh == 1 and kt == 1)
                        mm(PSO[:, bl * 256:(bl + 1) * 256], vtz[hh][:, tt, :], pTt[:, pi, kt * 256:(kt + 1) * 256], st_, sp_, [("vtm", tt), ("pT", pi)], PSOK)
                        mm(PSD[:, bl * 256:(bl + 1) * 256], onesz[hh][:], pTt[:, pi, kt * 256:(kt + 1) * 256], st_, sp_, ["onesz", ("pT", pi)], PSDK)
                else:
                    hf = tb
                    kti = t["kti"]
                    if kti < 6:
                        tt = 2 * hf + kti
                        vl = vtz[hh][:, tt, :]
                        vk = ("vtm", tt)
                    else:
                        kt = kti - 6
                        vl = ctxVz[hh][:, kt, cp, :]
                        vk = "ctxVz"
                    st_ = (hh == 0 and kti == 0)
                    sp_ = (hh == 1 and kti == 7)
                    mm(PSO[:, :], vl, pTt[:, pi, :], st_, sp_, [vk, ("pT", pi)], PSOK)
                    mm(PSD[:, :], onesz[hh][:], pTt[:, pi, :], st_, sp_, ["onesz", ("pT", pi)], PSDK)
                if t["last"]:
                    act(rt_t[:], PSD[:], AF.Ln, PSDK, ["rt"])
                    act(rstd_t[:], rt_t[:], AF.Exp, ["rt"], ["rstd"], scale=-1.0)
                    dve(lambda e, cp=cp, tb=tb: e.tensor_tensor(out=Ua(cp, tb), in0=PSO[:], in1=rstd_t[:], op=ALU.mult), PSOK + ["rstd"], [Uk(cp, tb)])

            w2x = {}
            _sl = len(tilesA) // 8
            for hh in range(2):
                h_ = 2 * cp + hh
                cf_ = e_i * 16 + h_
                cb_ = e_i * 16 + 8 + h_
                w2x[(3 * hh + 1) * _sl] = (lambda hh=hh, cf_=cf_: act(w2[:, hh, :], dpp, AF.Exp, UK_T + ["lg"], [("w2", hh)], scale=lg[:, cf_:cf_ + 1]))
                w2x[(3 * hh + 2) * _sl] = (lambda hh=hh, cb_=cb_: act(w2tmp, dnn, AF.Exp, UK_T + ["lg"], ["w2tmp"], scale=lg[:, cb_:cb_ + 1]))
                w2x[(3 * hh + 3) * _sl] = (lambda hh=hh: dve(lambda e, hh=hh: e.tensor_tensor(out=w2[:, hh, :], in0=w2[:, hh, :], in1=w2tmp, op=ALU.add), [("w2", hh), "w2tmp"], [("w2", hh)]))
            run_pipeline(tilesA, a_stage1, a_stage2, extras=w2x)

            wt, wk = load_w([wie[:, 1536 + cp * 128:1536 + (cp + 1) * 128], wie[:, 2048 + cp * 128:2048 + (cp + 1) * 128],
                             wie[:, 2560 + cp * 128:2560 + (cp + 1) * 128], wie[:, 3072 + cp * 128:3072 + (cp + 1) * 128]], 8)
            if g == 1:
                proj_fm(wt, wk, 0, qraw, "qraw")
                proj_fm(wt, wk, 128, kraw, "kraw")
                rope(qraw, "qraw", [qT], "qT")
                rope(kraw, "kraw", kTz, "kT")
            else:
                proj_fm(wt, wk, 0, qT, "qT")
                proj_k(wt, wk, 128)
            proj_fm(wt, wk, 384, gsT, "gsT", AF.Silu)
            for tt, pp, ppk in proj_tm(wt, wk, False):
                if g == 0:
                    dve(lambda e, pp=pp, tt=tt: e.tensor_copy(out=ktm[:, tt, :], in_=pp[:, 0:128]), [ppk], [("ktm", tt)])
            tilesB = mk_tiles()

            def b_stage1(t):
                tb, hh = t["tb"], t["hh"]
                h = 2 * cp + hh
                pr = slice(64 * hh, 64 * hh + 64)
                cf = e_i * 16 + h
                cb = e_i * 16 + 8 + h
                pS, pSk = nextps()
                pi = npi()
                t["pi"] = pi
                if g == 0:
                    b = 2 * tb + t["bl"]
                    qs = slice(b * 256, (b + 1) * 256)
                    for kt in range(2):
                        ks = slice((2 * b + kt) * 128, (2 * b + kt + 1) * 128)
                        mm(pS[:, kt * 256:(kt + 1) * 256], kTz[hh][:, ks], qT[:, qs], True, True, [("kT", tb), ("qT", tb)], [pSk])
                    for kt in range(2):
                        s0x = XC - 128 * kt
                        dve(lambda e, pS=pS, pi=pi, kt=kt, s0x=s0x, hh=hh: e.scalar_tensor_tensor(
                            out=pTt[:, pi, kt * 256:(kt + 1) * 256], in0=pS[:, kt * 256:(kt + 1) * 256], scalar=0.125,
                            in1=w2[:, hh, s0x:s0x + 256], op0=ALU.mult, op1=ALU.mult), [pSk, ("w2", hh)], [("pT", pi)])
                else:
                    hf = tb
                    kti = t["kti"]
                    qs = slice(hf * 512, (hf + 1) * 512)
                    if hf == 0 and kti == 0:
                        act(decS[pr, 0, :], ef[pr, :], AF.Exp, UK_C + ["lg"], [("decS", hh)], scale=lg[pr, cf:cf + 1])
                        act(decS[pr, 1, :], ef[pr, :], AF.Exp, UK_C + ["lg", "nlg", "lg1025"], [("decS", hh)], scale=nlg[pr, cb:cb + 1], bias=lg1025[pr, cb:cb + 1])
                    tt = kti
                    ks = slice(tt * 128, (tt + 1) * 128)
                    mm(pS[:], kTz[hh][:, ks], qT[:, qs], True, True, [("kT", tt // 4), ("qT", tb)], [pSk])
                    s0x = 512 * hf - 128 * tt + XC
                    dve(lambda e, pS=pS, pi=pi, s0x=s0x, hh=hh: e.scalar_tensor_tensor(
                        out=pTt[:, pi, :], in0=pS[:], scalar=0.125, in1=w2[:, hh, s0x:s0x + 512], op0=ALU.mult, op1=ALU.mult),
                        [pSk, ("w2", hh)], [("pT", pi)])
                    if kti == 7:
                        for dr in range(2):
                            dve(lambda e, dr=dr, pr=pr, qs=qs: e.tensor_tensor(out=pst[dr][pr, :], in0=qT[pr, qs], in1=decS[pr, dr, qs], op=ALU.mult),
                                [("qT", tb), ("decS", hh), ("pst", dr)], [("pst", dr)])

            def b_stage2(t):
                tb, hh, pi = t["tb"], t["hh"], t["pi"]
                if g == 0:
                    bl = t["bl"]
                    b = 2 * tb + bl
                    for kt in range(2):
                        tt = 2 * b + kt
                        mm(PSO[:, bl * 256:(bl + 1) * 256], vtz[hh][:, tt, :], pTt[:, pi, kt * 256:(kt + 1) * 256], hh == 0 and kt == 0, hh == 1 and kt == 1,
                           [("vtm", tt), ("pT", pi)], PSOK)
                else:
                    kti = t["kti"]
                    tt = kti
                    mm(PSO[:, :], vtz[hh][:, tt, :], pTt[:, pi, :], hh == 0 and kti == 0, False, [("vtm", tt), ("pT", pi)], PSOK)
                    if t["last"]:
                        for dr in range(2):
                            mm(PSO[:, :], s0z[:, dr, cp, :], pst[dr][:, :], False, dr == 1, ["s0", ("pst", dr)], PSOK)
                if t["last"]:
                    act(osq, PSO[:], AF.Square, PSOK, ["osq"])
                    pa, pak = nextps()
                    mm(pa[:], bd64[:], osq, True, True, ["bd64", "osq"], [pak])
                    act(rt_t[:], pa[:], AF.Ln, [pak, "eps"], ["rt"], bias=eps_t[:, 0:1])
                    act(rstd_t[:], rt_t[:], AF.Exp, ["rt"], ["rstd"], scale=-0.5)
                    t0 = tmp_t[0]
                    dve(lambda e: e.tensor_tensor(out=t0[:], in0=PSO[:], in1=rstd_t[:], op=ALU.mult), PSOK + ["rstd"], [("tmp", 0)])
                    dve(lambda e, cp=cp, tb=tb: e.tensor_tensor(out=Ua(4 + cp, tb), in0=t0[:], in1=gsT[:, tb * 512:(tb + 1) * 512], op=ALU.mult),
                        [("tmp", 0), ("gsT", tb)], [Uk(4 + cp, tb)])

            run_pipeline(tilesB, b_stage1, b_stage2)
            if g == 0 and "noS" not in parts:
                for b in range(4):
                    ri = b % 2
                    pR, pRk = nextps()
                    for dr in range(2):
                        dve(lambda e, b=b, dr=dr, cp=cp: e.tensor_tensor(
                            out=kdec[:, 2 * b:2 * b + 2, :].rearrange("p t (h d) -> p t h d", d=64),
                            in0=ktm[:, 2 * b:2 * b + 2, :].rearrange("p t (h d) -> p t h d", d=64),
                            in1=decP[:, dr, :, 2 * cp:2 * cp + 2].unsqueeze(3).to_broadcast([128, 2, 2, 64]), op=ALU.mult),
                            [("ktm", 2 * b), ("ktm", 2 * b + 1), "decP"], [("kdec", b)])
                        for hh in range(2):
                            col = (dr * 2 + hh) * 64
                            for kt in range(2):
                                tt = 2 * b + kt
                                mm(pR[0:64, col:col + 64], kdec[:, tt, 64 * hh:64 * hh + 64], vtz[hh][:, tt, 64 * hh:64 * hh + 64], kt == 0, kt == 1,
                                   [("kdec", b), ("vtm", tt)], [pRk])
                    act(retst[ri][0:64, 0:256], pR[0:64, 0:256], AF.Copy, [pRk], [("retst", ri)])
                    for r_ in range(2):
                        dst = nret[b, e_i, r_, 2 * cp:2 * cp + 2].rearrange("h k v -> k h v")
                        dma_sp(dst, retst[ri][0:64, r_ * 128:(r_ + 1) * 128].rearrange("p (h v) -> p h v", h=2), [("retst", ri)], [], ("retst", ri))

        if g == 0:
            dve(lambda e: e.memset(kvst1[:, 0, 0:1], 0.0), [], [("kvst", 0), ("retst", 0), ("retst", 1)])
        if "noO" not in parts:
            outproj_postnorm(l, g, w_out_even[e_i], 8, lambda k, tb: (Ua(k, tb), Uk(k, tb)), 2)

    def odd_mixer(l, g):
        o_i = l // 2
        wio = w_in_odd[o_i]
        cv = Carver()
        f4 = lambda: cv.take(4096, F32)
        b2 = lambda: cv.take(2048, BF16)
        sig2 = [f4(), f4()]
        fT, rA, R, qs_ = f4(), f4(), f4(), f4()
        A2 = [f4(), f4()]
        qe2 = [b2(), b2()]
        ke2 = [b2(), b2()]
        kdtm2 = [cv.take(2048, BF16).rearrange("p (t d) -> p t d", d=128) for _ in range(2)]
        gsTS = [b2(), b2()]
        vtmS = [cv.take(2048, BF16).rearrange("p (t d) -> p t d", d=128) for _ in range(2)]
        attm2 = [cv.take(512, BF16).rearrange("p (t s) -> p t s", s=32) for _ in range(2)]
        Sst = [[cv.take(512, F32) for _ in range(2)] for _ in range(2)]
        NR = 4
        Sbf = [[cv.take(256, BF16) for _ in range(NR)] for _ in range(2)]
        hgst = [cv.take(512, F32) for _ in range(2)]
        oacc = Uu[:, 8192:8192 + 2048].bitcast(F32)
        kdT = Uu[:, 10240:10240 + 1024]
        osq = Uu[:, 11264:11264 + 512]
        C = 32
        nseq = 4 if g == 0 else 1
        T = 256 if g == 0 else 1024
        ncs = T // C
        NST = 1024 // C
        dve(lambda e: e.memset(epoch_t[:], 0.0), [], ["arena_epoch"])
        Rb = R.bitcast(BF16)
        Rm = [Rb[:, 0:1024], Rb[:, 1024:2048]]
        dve(lambda e: e.memset(Rb, 1.0), [], ["R"])
        dve(lambda e: e.memset(Rm[0].rearrange("p (c t) -> p c t", t=C)[:, :, 0:1], 0.0), ["R"], ["R"])
        dve(lambda e: e.memset(Rm[1].rearrange("p (c t) -> p c t", t=C)[:, :, C - 1:C], 0.0), ["R"], ["R"])
        _hg = [0]

        def ps3():
            i = _rr[0] % 3
            _rr[0] += 1
            return ps[i], ("ps", i)

        def load_slot(i, pieces, kc=8):
            ntot = sum(p.shape[1] for p in pieces)
            view = wbuf[i][:, 0:kc * ntot].rearrange("p (k n) -> p k n", n=ntot)
            off = 0
            for p in pieces:
                n = p.shape[1]
                dma_pool(view[:, :, off:off + n], p.rearrange("(k p) n -> p k n", p=128), [], [("wb", i)], ("wb", i))
                off += n
            return view, ("wb", i)

        Wt = {}

        def load_head(h):
            wt, wk = load_slot(h % 2, [wio[:, h * 128:(h + 1) * 128], wio[:, 1024 + h * 128:1024 + (h + 1) * 128], wio[:, 2048 + h * 128:2048 + (h + 1) * 128]])
            wt2, wk2 = load_slot(2, [wio[:, 3072 + h * 128:3072 + (h + 1) * 128], wio[:, 4096 + h * 128:4096 + (h + 1) * 128]])
            Wt[h] = (wt, wk, wt2, wk2)

        def fm_thunks(h, which):
            wt, wk, wt2, wk2 = Wt[h]
            out = []
            for tb in range(2):
                def th(tb=tb):
                    sl = slice(tb * 512, (tb + 1) * 512)
                    pp, ppk = ps3()
                    if which == "q":
                        w_, wk_, col, dst, func, key = wt, wk, 0, qs_, AF.Silu, ("qs", tb)
                    elif which == "g":
                        w_, wk_, col, dst, func, key = wt2, wk2, 128, gsTS[h % 2], AF.Silu, ("gsT", h % 2, tb)
                    else:
                        dr = which
                        w_, wk_, col, dst, func, key = wt, wk, 128 * (1 + dr), sig2[dr], AF.Sigmoid, ("sig", dr, tb)
                    for k in range(8):
                        mm(pp[:], w_[:, k, col:col + 128], Ha(k, tb), k == 0, k == 7, [wk_, Hk(k, tb)], [ppk])
                    act(dst[:, sl], pp[:], func, [ppk], [key])
                out.append(th)
            return out

        def v_thunks(h):
            wt, wk, wt2, wk2 = Wt[h]
            vtm = vtmS[h % 2]
            st_ = {}
            out = []
            for tt in range(8):
                def th(tt=tt):
                    tb = tt // 4
                    if tt % 4 == 0:
                        st_["pp"] = ps3()
                    pp, ppk = st_["pp"]
                    for k in range(8):
                        mm(pp[:, (tt % 4) * 128:(tt % 4 + 1) * 128], H_bf[:, k, tt * 128:(tt + 1) * 128], wt2[:, k, 0:128], k == 0, k == 7, [wk2, Hk(k, tb)], [ppk])
                    if tt % 4 == 3:
                        act(vtm[:, tt - 3:tt + 1, :], pp[:].rearrange("p (t d) -> p t d", d=128), AF.Copy, [ppk], [("vtm", h % 2, tb)])
                out.append(th)
            return out

        def chain_ops(h, dr):
            A, qe, ke = A2[dr], qe2[dr], ke2[dr]
            sig = sig2[dr]
            Ak, qek, kek = ("A", dr), ("qe", dr), ("ke", dr)
            lo = lowt[:, o_i, dr, h:h + 1]
            om = omlt[:, o_i, dr, h:h + 1]
            nom = nomlt[:, o_i, dr, h:h + 1]
            sgk = [("sig", dr, 0), ("sig", dr, 1)]
            Av = A.rearrange("p (c t) -> p c t", t=C)
            alast = Av[:, :, C - 1:C] if dr == 0 else Av[:, :, 0:1]
            ops = []
            ops.append(lambda: dve(lambda e: e.tensor_scalar(fT, sig, om, lo, op0=ALU.mult, op1=ALU.add), sgk + ["omlt", "lowt"], ["fT"]))
            ops.append(lambda: dve(lambda e: e.tensor_scalar(sig, sig, nom, om, op0=ALU.mult, op1=ALU.add), sgk + ["omlt", "nomlt"], sgk))
            ops.append(lambda: act(fT, fT, AF.Ln, ["fT"], ["fT"]))
            if dr == 0:
                ops.append(lambda: dve(lambda e: e.tensor_tensor_scan(out=rA, data0=Rm[0], data1=fT, initial=0.0, op0=ALU.mult, op1=ALU.add), ["fT", "R"], ["rA"]))
            else:
                ops.append(lambda: dve(lambda e: e.tensor_tensor_scan(out=rA[:, ::-1], data0=Rm[1][:, ::-1], data1=fT[:, ::-1], initial=0.0, op0=ALU.mult, op1=ALU.add), ["fT", "R"], ["rA"]))
            ops.append(lambda: act(A, rA, AF.Exp, ["rA"], [Ak]))
            ops.append(lambda: dve(lambda e: e.tensor_scalar_max(rA, rA, -69.0), ["rA"], ["rA"]))
            ops.append(lambda: act(rA, rA, AF.Exp, ["rA"], ["rA"], scale=-1.0))
            ops.append(lambda: dve(lambda e: e.scalar_tensor_tensor(out=qe, in0=qs_, scalar=128.0 ** -0.5, in1=A, op0=ALU.mult, op1=ALU.mult), [("qs", 0), ("qs", 1), Ak], [qek]))
            ops.append(lambda: dve(lambda e: e.tensor_tensor(out=sig, in0=sig, in1=rA, op=ALU.mult), sgk + ["rA"], sgk))
            ops.append(lambda: dve(lambda e: e.tensor_copy(out=ke, in_=sig), sgk, [kek]))
            ops.append(lambda: dve(lambda e: e.tensor_tensor(out=kdT.rearrange("p (c t) -> p c t", t=C), in0=sig.rearrange("p (c t) -> p c t", t=C),
                                                             in1=alast.to_broadcast([128, 1024 // C, C]), op=ALU.mult), sgk + [Ak], ["kdT"]))
            return ops

        def post_chain(h, dr):
            qe, ke, kdtm, attm = qe2[dr], ke2[dr], kdtm2[dr], attm2[dr]
            qek, kek, kdk, atk = ("qe", dr), ("ke", dr), ("kdtm", dr), ("attm", dr)
            for tt in range(8):
                P.op("pe", lambda e, tt=tt: e.transpose(out=psb[:, tt * 128:(tt + 1) * 128], in_=kdT[:, tt * 128:(tt + 1) * 128], identity=ident_b[:]),
                     reads=["kdT", "identb"], writes=["psb"])
            act(kdtm, psb[:].rearrange("p (t d) -> p t d", d=128), AF.Copy, ["psb"], [kdk])
            pA, pAk = ps3()
            for c in range(NST):
                tt = c // 4
                q4 = 32 * (c % 4)
                cs = slice(c * C, (c + 1) * C)
                mm(pA[q4:q4 + 32, tt * 32:(tt + 1) * 32], ke[:, cs], qe[:, cs], True, True, [kek, qek], [pAk], tp=(0, q4))
            dve(lambda e, pA=pA, dr=dr, attm=attm: e.tensor_tensor(out=attm, in0=pA[:, 0:256].rearrange("p (t s) -> p t s", s=32),
                                                                      in1=hmask[:, dr, :].unsqueeze(1).to_broadcast([128, 8, 32]), op=ALU.mult),
                [pAk, "hmask"], [atk])

        def interleave(chain, hoisted, positions):
            j = 0
            for ci, op_ in enumerate(chain):
                op_()
                while j < len(hoisted) and positions[j] <= ci:
                    hoisted[j]()
                    j += 1
            while j < len(hoisted):
                hoisted[j]()
                j += 1

        load_head(0)
        for th in fm_thunks(0, "q") + fm_thunks(0, "g") + v_thunks(0) + fm_thunks(0, 0):
            th()
        for h in range(8):
            nxt = h + 1 < 8
            if nxt:
                load_head(h + 1)
            hoistA = fm_thunks(h, 1) + ((fm_thunks(h + 1, "g") + v_thunks(h + 1)) if nxt else [])
            interleave(chain_ops(h, 0), hoistA, [0, 0] + [1 + (i * 9) // 10 for i in range(10)])
            post_chain(h, 0)
            hoistB = (fm_thunks(h + 1, 0) + fm_thunks(h + 1, "q")) if nxt else []
            interleave(chain_ops(h, 1), hoistB, [0, 1, 7, 8])
            post_chain(h, 1)

            def chunk_of(dr, i):
                b, ci = divmod(i, ncs)
                return b * ncs + (ci if dr == 0 else ncs - 1 - ci)

            psb_f = psb[:].bitcast(F32)
            ubanks = [(ps[0][:, 0:128], ("ps", 0)), (ps[1][:, 0:128], ("ps", 1)), (ps[2][:, 0:128], ("ps", 2)), (psb_f[:, 0:128], "psb")]

            def emit_U(i):
                out = []
                for dr in range(2):
                    c = chunk_of(dr, i)
                    tt = c // 4
                    q4 = 32 * (c % 4)
                    pU, pUk = ubanks[(2 * i + dr) % 4]
                    mm(pU, kdtm2[dr][q4:q4 + 32, tt, :], vtmS[h % 2][q4:q4 + 32, tt, :], True, True, [("kdtm", dr), ("vtm", h % 2, tt // 4)], [pUk], tp=(q4, 0))
                    out.append((pU, pUk))
                return out

            if g == 1:
                for dr in range(2):
                    dma_sp(Sst[dr][1], shg[o_i, dr, h], ["arena_epoch"], [("S", dr, 1)], ("S", dr))
                    act(Sbf[dr][NR - 1], Sst[dr][1], AF.Copy, [("S", dr, 1)], [("Sbf", dr, NR - 1)])
            LOOK = 1
            dve(lambda e: e.memset(oacc, 0.0), [("oacc", j) for j in range(4)], [("oacc", j) for j in range(4)])
            pend = {}
            for i in range(min(LOOK, NST)):
                pend[i] = emit_U(i)
            for i in range(NST):
                if i + LOOK < NST:
                    pend[i + LOOK] = emit_U(i + LOOK)
                b, ci = divmod(i, ncs)
                first = (ci == 0) and g == 0
                for dr in range(2):
                    c = chunk_of(dr, i)
                    tt = c // 4
                    q4 = 32 * (c % 4)
                    cs = slice(c * C, (c + 1) * C)
                    bank = (3 if dr == 0 else 5) + ((i // 8) % 2)
                    pO, pOk = ps[bank], ("ps", bank)
                    oc = slice((i % 8) * C, (i % 8 + 1) * C)
                    mm(pO[:, oc], vtmS[h % 2][q4:q4 + 32, tt, :], attm2[dr][q4:q4 + 32, tt, :], True, first, [("vtm", h % 2, tt // 4), ("attm", dr)], [pOk], tp=(q4, 0))
                    if not first:
                        mm(pO[:, oc], Sbf[dr][(i - 1) % NR], qe2[dr][:, cs], False, True, [("Sbf", dr, (i - 1) % NR), ("qe", dr)], [pOk])
                for dr in range(2):
                    c = chunk_of(dr, i)
                    pU, pUk = pend[i][dr]
                    A = A2[dr]
                    al = A[:, c * C + C - 1:c * C + C] if dr == 0 else A[:, c * C:c * C + 1]
                    cur, prv = Sst[dr][i % 2], Sst[dr][(i - 1) % 2]
                    if first:
                        dve(lambda e, cur=cur, pU=pU: e.tensor_copy(out=cur, in_=pU), [pUk], [("S", dr, i % 2)])
                    else:
                        dve(lambda e, cur=cur, prv=prv, pU=pU, al=al: e.scalar_tensor_tensor(out=cur, in0=prv, scalar=al, in1=pU, op0=ALU.mult, op1=ALU.add),
                            [pUk, ("S", dr, (i - 1) % 2), ("A", dr)], [("S", dr, i % 2)])
                    last = (ci == ncs - 1)
                    if not (last and (g == 0 or i == NST - 1)):
                        act(Sbf[dr][i % NR], cur, AF.Copy, [("S", dr, i % 2)], [("Sbf", dr, i % NR)])
                    if last and g == 0:
                        hi = _hg[0] % 2
                        _hg[0] += 1
                        act(hgst[hi], cur, AF.Copy, [("S", dr, i % 2)], [("hgst", hi)])
                        dma_sp(nhg[b, o_i, dr, h], hgst[hi], [("hgst", hi)], [], ("hgst", hi))
                del pend[i]
                if i % 8 == 7:
                    for dr in range(2):
                        bank = (3 if dr == 0 else 5) + ((i // 8) % 2)
                        pO, pOk = ps[bank], ("ps", bank)
                        c_first = chunk_of(dr, i - 7)
                        c_last = chunk_of(dr, i)
                        lo_c = min(c_first, c_last)
                        osl = slice(lo_c * C, (lo_c + 8) * C)
                        rng_ = lo_c // 8
                        srcv = pO[:, 0:8 * C].rearrange("p (c t) -> p c t", t=C)
                        if dr == 1:
                            srcv = srcv[:, ::-1, :]
                        if True:
                            dve(lambda e, osl=osl, srcv=srcv: e.tensor_tensor(out=oacc[:, osl].rearrange("p (c t) -> p c t", t=C), in0=oacc[:, osl].rearrange("p (c t) -> p c t", t=C),
                                                                             in1=srcv, op=ALU.add), [pOk, ("oacc", lo_c // 8)], [("oacc", lo_c // 8)])
            for tb in range(2):
                sl = slice(tb * 512, (tb + 1) * 512)
                oak = [("oacc", j) for j in range(4)]
                act(osq, oacc[:, sl], AF.Square, oak, ["osq"])
                pa, pak = ps3()
                mm(pa[:], om128[:], osq, True, True, ["om128", "osq"], [pak])
                act(rt_t[:], pa[:], AF.Ln, [pak, "eps"], ["rt"], bias=eps_t[:, 0:1])
                act(rstd_t[:], rt_t[:], AF.Exp, ["rt"], ["rstd"], scale=-0.5)
                t0 = tmp_t[0]
                dve(lambda e, sl=sl: e.scalar_tensor_tensor(out=t0[:], in0=oacc[:, sl], scalar=gnT[:, o_i:o_i + 1], in1=rstd_t[:], op0=ALU.mult, op1=ALU.mult),
                    oak + ["rstd", "gnT"], [("tmp", 0)])
                dve(lambda e, sl=sl, h=h, tb=tb: e.tensor_tensor(out=Ua(h, tb), in0=t0[:], in1=gsTS[h % 2][:, sl], op=ALU.mult), [("tmp", 0), ("gsT", h % 2, tb)], [Uk(h, tb)])
        if g == 0:
            dve(lambda e: e.memset(hgst[0][:, 0:1], 0.0), [], [("hgst", 0), ("hgst", 1)])
        outproj_postnorm(l, g, w_out_odd[o_i], 8, lambda k, tb: (Ua(k, tb), Uk(k, tb)), 2)

    for gi_, g in enumerate(groups):
        load_group(g)
        if gi_ == 0 and "mod" in parts:
            for mq in range(12):
                mod_tile(0, mq)
            mod_finish(0)
        for l in range(depth):
            if "pre" in parts:
                prenorm(l, 0, g)
            if "mixer" in parts:
                if l % 2 == 0:
                    even_mixer(l, g)
                else:
                    odd_mixer(l, g)
            if "pre" in parts:
                prenorm(l, 1, g)
            if "ffn" in parts:
                ffn(l, g)
        store_group(g)

    P.emit()
    st.close()
    global LASTP
    LASTP = P
    return nc


_CACHE = {}


def make_in_maps(inputs):
    f = lambda a: np.ascontiguousarray(np.asarray(a, dtype=np.float32))
    hcs = {"c_" + k: v for k, v in _host_consts().items()}
    nabt = _nabias(f(inputs["rpb"]))
    shared = {k: f(inputs[k]) for k in ["w_mod", "b_mod", "norm_g", "w_in_even", "w_out_even", "w_in_odd", "w_out_odd",
                                       "hgrn_lb", "hgrn_gnorm", "w_ffn_in", "w_ffn_out"]}
    shared["ret_decay"] = f(inputs["ret_decay"]).reshape(32)
    shared["nab"] = nabt
    shared.update(hcs)
    xp = f(inputs["x_prompt"])
    xs = f(inputs["x_sample"])
    maps = []
    for i in range(8):
        m = dict(shared)
        m["xin"] = np.ascontiguousarray(np.stack([xp[4 * i:4 * i + 4].reshape(1024, D), xs[i]], axis=0))
        m["ckv"] = f(inputs["cache_kv"][i])
        m["sret"] = f(inputs["state_ret"][i])
        m["shg"] = f(inputs["state_hgrn"][i])
        m["cvec"] = np.ascontiguousarray(np.stack([f(inputs["c_ctx"]), f(inputs["c"][i])], axis=0))
        maps.append(m)
    return maps


def kernel(**inputs):
    if "nc" not in _CACHE:
        _CACHE["nc"] = build()
    nc = _CACHE["nc"]
    maps = make_in_maps(inputs)
    res = run_bass_kernel_spmd(nc, maps, core_ids=list(range(8)))
    r = res.results
    y = np.stack([x["yout"] for x in r], axis=0)
    y_prompt = y[:, 0].reshape(32, 256, D)
    y_sample = y[:, 1]
    nkv = np.concatenate([x["nkv"] for x in r], axis=0)
    nret = np.concatenate([x["nret"] for x in r], axis=0)
    nhg = np.concatenate([x["nhg"] for x in r], axis=0)
    return (np.ascontiguousarray(y_prompt, dtype=np.float32), np.ascontiguousarray(y_sample, dtype=np.float32),
            np.ascontiguousarray(nkv, dtype=np.float32), np.ascontiguousarray(nret, dtype=np.float32),
            np.ascontiguousarray(nhg, dtype=np.float32))
```
